# Optimizing a Trainium2 kernel written in Bass

```python
import math
import jax, jax.numpy as jnp
from jax import lax
import numpy as np

D_MODEL = 1024
BATCH = 8
SEQ = 2048
DEPTH = 2

N_A_LAYERS = DEPTH // 2
N_B_LAYERS = DEPTH - N_A_LAYERS
EPS = 1e-5

SSM_EXPAND = 2
SSM_INNER = SSM_EXPAND * D_MODEL
SSM_HEAD_DIM = 64
SSM_HEADS = SSM_INNER // SSM_HEAD_DIM
SSM_GROUPS = 8
SSM_STATE = 128
SSM_CONV = 4
SSM_CHUNK = 256
SSM_CONV_DIM = SSM_INNER + 2 * SSM_GROUPS * SSM_STATE
SSM_PROJ = 2 * SSM_INNER + 2 * SSM_GROUPS * SSM_STATE + SSM_HEADS

ATT_HEAD_DIM = 64
ATT_Q_HEADS = D_MODEL // ATT_HEAD_DIM
ATT_KV_HEADS = 4
ATT_GROUP = ATT_Q_HEADS // ATT_KV_HEADS
WINDOW = 128
ROPE_THETA = 10000.0

FFN_DIM = 2816
FFN_CONV = 3

kernel_name = "yoco_mamba2_swa_sink_convffn"


def rms_norm(x, g):
    xf = x.astype(jnp.float32)
    xf = xf * lax.rsqrt(jnp.mean(xf * xf, axis=-1, keepdims=True) + EPS)
    return xf.astype(x.dtype) * g


def group_rms_norm(y, g, groups):
    b, s, c = y.shape
    yf = y.astype(jnp.float32).reshape(b, s, groups, c // groups)
    yf = yf * lax.rsqrt(jnp.mean(yf * yf, axis=-1, keepdims=True) + EPS)
    return yf.reshape(b, s, c).astype(y.dtype) * g


def causal_dwconv(x, w, bias):
    width, ch = w.shape
    out = lax.conv_general_dilated(
        x, w[:, None, :].astype(x.dtype), window_strides=(1,),
        padding=[(width - 1, 0)], dimension_numbers=("NWC", "WIO", "NWC"),
        feature_group_count=ch)
    return out + bias


def rotary(x, positions):
    half = x.shape[-1] // 2
    inv_freq = ROPE_THETA ** (-jnp.arange(half, dtype=jnp.float32) / half)
    ang = positions.astype(jnp.float32)[..., None] * inv_freq
    cos = jnp.cos(ang)[:, :, None, :]
    sin = jnp.sin(ang)[:, :, None, :]
    xf = x.astype(jnp.float32)
    x1, x2 = xf[..., :half], xf[..., half:]
    return jnp.concatenate([x1 * cos - x2 * sin, x2 * cos + x1 * sin], axis=-1).astype(x.dtype)


def ssd_chunked(x, dt, A, Bm, Cm):
    b, s, h, p = x.shape
    g, n = Bm.shape[2], Bm.shape[3]
    e = h // g
    pad = (-s) % SSM_CHUNK
    x = jnp.pad(x, ((0, 0), (0, pad), (0, 0), (0, 0)))
    dt = jnp.pad(dt, ((0, 0), (0, pad), (0, 0)))
    Bm = jnp.pad(Bm, ((0, 0), (0, pad), (0, 0), (0, 0)))
    Cm = jnp.pad(Cm, ((0, 0), (0, pad), (0, 0), (0, 0)))
    L = SSM_CHUNK
    c = (s + pad) // L
    xd = (x * dt[..., None]).reshape(b, c, L, g, e, p)
    a = (dt * A).reshape(b, c, L, g, e)
    a_cum = jnp.cumsum(a, axis=2)
    Bc = Bm.reshape(b, c, L, g, n)
    Cc = Cm.reshape(b, c, L, g, n)
    seg = a_cum[:, :, :, None] - a_cum[:, :, None, :]
    causal = jnp.tril(jnp.ones((L, L), dtype=bool))[None, None, :, :, None, None]
    decay = jnp.exp(jnp.where(causal, seg, -jnp.inf))
    cb = jnp.einsum("bclgn,bcsgn->bclsg", Cc, Bc)
    w = cb[..., None] * decay
    y_diag = jnp.einsum("bclsge,bcsgep->bclgep", w, xd)
    decay_to_end = jnp.exp(a_cum[:, :, -1:] - a_cum)
    states = jnp.einsum("bclgn,bclge,bclgep->bcgepn", Bc, decay_to_end, xd)
    chunk_decay = jnp.exp(a_cum[:, :, -1])

    def step(state, inp):
        dec, new = inp
        return state * dec[..., None, None] + new, state

    init = jnp.zeros((b, g, e, p, n), jnp.float32)
    _, prev = lax.scan(step, init, (jnp.moveaxis(chunk_decay, 1, 0), jnp.moveaxis(states, 1, 0)))
    prev = jnp.moveaxis(prev, 0, 1)
    y_off = jnp.einsum("bclgn,bcgepn->bclgep", Cc, prev) * jnp.exp(a_cum)[..., None]
    y = (y_diag + y_off).reshape(b, c * L, h, p)
    return y[:, :s]


def mamba2_mixer(h, in_proj, conv_w, conv_b, dt_bias, A_log, D, gnorm, out_proj):
    b, s, _ = h.shape
    zxbcdt = h @ in_proj
    z, xBC, dt = jnp.split(zxbcdt, [SSM_INNER, SSM_INNER + SSM_CONV_DIM], axis=-1)
    xBC = jax.nn.silu(causal_dwconv(xBC, conv_w, conv_b))
    xs, Bm, Cm = jnp.split(xBC, [SSM_INNER, SSM_INNER + SSM_GROUPS * SSM_STATE], axis=-1)
    xs = xs.reshape(b, s, SSM_HEADS, SSM_HEAD_DIM).astype(jnp.float32)
    Bm = Bm.reshape(b, s, SSM_GROUPS, SSM_STATE).astype(jnp.float32)
    Cm = Cm.reshape(b, s, SSM_GROUPS, SSM_STATE).astype(jnp.float32)
    dt = jax.nn.softplus(dt.astype(jnp.float32) + dt_bias.astype(jnp.float32))
    A = -jnp.exp(A_log.astype(jnp.float32))
    y = ssd_chunked(xs, dt, A, Bm, Cm) + xs * D.astype(jnp.float32)[:, None]
    y = y.reshape(b, s, SSM_INNER).astype(h.dtype)
    y = group_rms_norm(y * jax.nn.silu(z), gnorm, SSM_GROUPS)
    return y @ out_proj


def sliding_window_sink_attention(q, k, v, sinks):
    b, s, _, d = q.shape
    nb = s // WINDOW
    qb = q.reshape(b, nb, WINDOW, ATT_KV_HEADS, ATT_GROUP, d)
    kb = k.reshape(b, nb, WINDOW, ATT_KV_HEADS, d)
    vb = v.reshape(b, nb, WINDOW, ATT_KV_HEADS, d)
    blk_pad = ((0, 0), (1, 0), (0, 0), (0, 0), (0, 0))
    k_band = jnp.concatenate([jnp.pad(kb, blk_pad)[:, :-1], kb], axis=2)
    v_band = jnp.concatenate([jnp.pad(vb, blk_pad)[:, :-1], vb], axis=2)
    scores = jnp.einsum("bnqhgd,bnkhd->bnhgqk", qb, k_band).astype(jnp.float32) * (d ** -0.5)
    qi = jnp.arange(WINDOW)[:, None]
    ki = jnp.arange(2 * WINDOW)[None, :]
    rel = qi + WINDOW - ki
    kpos = jnp.arange(nb)[:, None, None] * WINDOW + ki[None] - WINDOW
    mask = (rel >= 0)[None] & (rel < WINDOW)[None] & (kpos >= 0)
    scores = jnp.where(mask[None, :, None, None], scores, -jnp.inf)
    sink = jnp.broadcast_to(
        sinks.astype(jnp.float32).reshape(ATT_KV_HEADS, ATT_GROUP)[None, None, :, :, None, None],
        scores.shape[:-1] + (1,))
    probs = jax.nn.softmax(jnp.concatenate([scores, sink], axis=-1), axis=-1)[..., :-1]
    out = jnp.einsum("bnhgqk,bnkhd->bnqhgd", probs.astype(v.dtype), v_band)
    return out.reshape(b, s, ATT_Q_HEADS * d)


def conv_ffn(x, norm_g, w_in, conv_w, conv_b, w_down):
    h = rms_norm(x, norm_g)
    gate, val = jnp.split(h @ w_in, [FFN_DIM], axis=-1)
    gate = causal_dwconv(gate, conv_w, conv_b)
    return (jax.nn.silu(gate) * val) @ w_down


def _dense(key, shape, fan_in):
    return jax.random.normal(key, shape, jnp.float32) * (fan_in ** -0.5)


def _gain(key, shape):
    return 1.0 + 0.02 * jax.random.normal(key, shape, jnp.float32)


def _small(key, shape):
    return 0.02 * jax.random.normal(key, shape, jnp.float32)


def setup_inputs(seed: int = 0) -> dict:
    key = jax.random.key(seed)
    ks = jax.random.split(key, 32)
    qkv_dim = ATT_KV_HEADS * ATT_HEAD_DIM
    q_dim = ATT_Q_HEADS * ATT_HEAD_DIM
    x = jax.random.normal(ks[0], (BATCH, SEQ, D_MODEL), jnp.float32)
    positions = (jnp.arange(SEQ, dtype=jnp.int32)[None, :]
                 + jax.random.randint(ks[1], (BATCH, 1), 0, 4096, dtype=jnp.int32))
    dt0 = jnp.exp(jax.random.uniform(ks[6], (N_A_LAYERS, SSM_HEADS), jnp.float32,
                                     minval=math.log(1e-3), maxval=math.log(1e-1)))
    return {
        "x": x,
        "positions": positions,
        "a_norm": _gain(ks[2], (N_A_LAYERS, D_MODEL)),
        "a_in_proj": _dense(ks[3], (N_A_LAYERS, D_MODEL, SSM_PROJ), D_MODEL),
        "a_conv_w": jax.random.normal(ks[4], (N_A_LAYERS, SSM_CONV, SSM_CONV_DIM), jnp.float32) * (SSM_CONV ** -0.5),
        "a_conv_b": _small(ks[5], (N_A_LAYERS, SSM_CONV_DIM)),
        "a_dt_bias": dt0 + jnp.log(-jnp.expm1(-dt0)),
        "a_A_log": jnp.log(jax.random.uniform(ks[7], (N_A_LAYERS, SSM_HEADS), jnp.float32, minval=1.0, maxval=16.0)),
        "a_D": 1.0 + 0.1 * jax.random.normal(ks[8], (N_A_LAYERS, SSM_HEADS), jnp.float32),
        "a_gnorm": _gain(ks[9], (N_A_LAYERS, SSM_INNER)),
        "a_out_proj": _dense(ks[10], (N_A_LAYERS, SSM_INNER, D_MODEL), SSM_INNER),
        "kv_norm": _gain(ks[11], (D_MODEL,)),
        "w_kv": _dense(ks[12], (D_MODEL, 2 * qkv_dim), D_MODEL),
        "b_kv": _small(ks[13], (2 * qkv_dim,)),
        "k_norm": _gain(ks[14], (ATT_HEAD_DIM,)),
        "b_norm": _gain(ks[15], (N_B_LAYERS, D_MODEL)),
        "w_q": _dense(ks[16], (N_B_LAYERS, D_MODEL, q_dim), D_MODEL),
        "b_q": _small(ks[17], (N_B_LAYERS, q_dim)),
        "q_norm": _gain(ks[18], (N_B_LAYERS, ATT_HEAD_DIM)),
        "sinks": jax.random.normal(ks[19], (N_B_LAYERS, ATT_Q_HEADS), jnp.float32),
        "w_o": _dense(ks[20], (N_B_LAYERS, q_dim, D_MODEL), q_dim),
        "b_o": _small(ks[21], (N_B_LAYERS, D_MODEL)),
        "f_norm": _gain(ks[22], (DEPTH, D_MODEL)),
        "f_w_in": _dense(ks[23], (DEPTH, D_MODEL, 2 * FFN_DIM), D_MODEL),
        "f_conv_w": jax.random.normal(ks[24], (DEPTH, FFN_CONV, FFN_DIM), jnp.float32) * (FFN_CONV ** -0.5),
        "f_conv_b": _small(ks[25], (DEPTH, FFN_DIM)),
        "f_w_down": _dense(ks[26], (DEPTH, FFN_DIM, D_MODEL), FFN_DIM),
    }


def reference(x, positions, a_norm, a_in_proj, a_conv_w, a_conv_b, a_dt_bias, a_A_log, a_D,
              a_gnorm, a_out_proj, kv_norm, w_kv, b_kv, k_norm, b_norm, w_q, b_q, q_norm,
              sinks, w_o, b_o, f_norm, f_w_in, f_conv_w, f_conv_b, f_w_down):
    b, s, _ = x.shape
    k_shared = None
    v_shared = None
    for layer in range(DEPTH):
        if layer < N_A_LAYERS:
            i = layer
            h = rms_norm(x, a_norm[i])
            x = x + mamba2_mixer(h, a_in_proj[i], a_conv_w[i], a_conv_b[i], a_dt_bias[i],
                                 a_A_log[i], a_D[i], a_gnorm[i], a_out_proj[i])
        else:
            i = layer - N_A_LAYERS
            if i == 0:
                kv = rms_norm(x, kv_norm) @ w_kv + b_kv
                k_shared, v_shared = jnp.split(kv, 2, axis=-1)
                k_shared = k_shared.reshape(b, s, ATT_KV_HEADS, ATT_HEAD_DIM)
                v_shared = v_shared.reshape(b, s, ATT_KV_HEADS, ATT_HEAD_DIM)
                k_shared = rotary(rms_norm(k_shared, k_norm), positions)
            h = rms_norm(x, b_norm[i])
            q = (h @ w_q[i] + b_q[i]).reshape(b, s, ATT_Q_HEADS, ATT_HEAD_DIM)
            q = rotary(rms_norm(q, q_norm[i]), positions)
            att = sliding_window_sink_attention(q, k_shared, v_shared, sinks[i])
            x = x + att @ w_o[i] + b_o[i]
        x = x + conv_ffn(x, f_norm[layer], f_w_in[layer], f_conv_w[layer], f_conv_b[layer], f_w_down[layer])
    return x
```

```python
import math
from contextlib import ExitStack

import numpy as np
import ml_dtypes
import concourse.bass as bass
import concourse.mybir as mybir
from concourse.bass_utils import run_bass_kernel_spmd

F32 = mybir.dt.float32
BF16 = mybir.dt.bfloat16
I32 = mybir.dt.int32
AF = mybir.ActivationFunctionType
OP = mybir.AluOpType
DT_SIZE = {F32: 4, BF16: 2, I32: 4}

T = 2048
D = 1024
TB = 512
NBLK = T // TB
EPS = 1e-5
SSM_INNER = 2048
NHEAD = 32
FFN = 2816
NFC = FFN // 128
SAME_ENGINE_SYNC = True
NSLOT = 5
INP_ORDER = [4, 5, 0, 6, 7, 1, 8, 9, 2, 10, 11, 3]
FDN_PARTS = [(0, 6), (6, 6), (12, 5), (17, 5)]
SLOT_ELEMS = 8 * 512


def _cst_layout():
    lay = {}
    off = 0

    def add(name, n):
        nonlocal off
        lay[name] = (off, n)
        off += n

    for nm in ["a_norm", "f_norm0", "f_norm1", "kv_norm", "b_norm"]:
        add(nm, 8)
    for k in range(4):
        add(f"cw{k}", 32)
    add("cb", 32)
    add("dtb3", 1)
    add("alog3", 1)
    add("Dc", 16)
    add("gn", 16)
    for l in range(2):
        for k in range(3):
            add(f"fcw{l}_{k}", NFC)
        add(f"fcb{l}", NFC)
    add("bk", 2)
    add("kn", 1)
    add("qn", 1)
    add("bq", 8)
    add("invf", 1)
    add("sgn", 1)
    add("sink", 8)
    add("bo", 8)
    add("bv", 256)
    return lay, off


CST, NCST = _cst_layout()
CM = {"ident": (0, 128), "ones": (128, 128), "bd64": (256, 128), "perm": (384, 128),
      "mcur": (512, 512), "mprev": (1024, 512), "m3": (1536, 32)}
NCM = 1536 + 32
CF = {"identf": (0, 128), "sel127": (128, 128)}
NCF = 256


def _fm(v, nch):
    return np.ascontiguousarray(np.asarray(v, np.float32).reshape(nch, 128).T)


def _host_consts(inp):
    cst = np.zeros((128, NCST), np.float32)

    def put(name, arr):
        o, n = CST[name]
        cst[:, o:o + n] = np.asarray(arr, np.float32).reshape(128, n)

    put("a_norm", _fm(inp["a_norm"][0], 8))
    put("f_norm0", _fm(inp["f_norm"][0], 8))
    put("f_norm1", _fm(inp["f_norm"][1], 8))
    put("kv_norm", _fm(inp["kv_norm"], 8))
    put("b_norm", _fm(inp["b_norm"][0], 8))
    for k in range(4):
        put(f"cw{k}", _fm(inp["a_conv_w"][0][k], 32))
    put("cb", _fm(inp["a_conv_b"][0], 32))
    p = np.arange(128)
    v = np.zeros(128, np.float32)
    v[:96] = np.tile(inp["a_dt_bias"][0], 3)
    put("dtb3", v)
    v = np.zeros(128, np.float32)
    v[:96] = np.tile(inp["a_A_log"][0], 3)
    put("alog3", v)
    Dv = inp["a_D"][0]
    put("Dc", np.stack([Dv[2 * c + p // 64] for c in range(16)], axis=1))
    put("gn", _fm(inp["a_gnorm"][0], 16))
    for l in range(2):
        for k in range(3):
            put(f"fcw{l}_{k}", _fm(inp["f_conv_w"][l][k], NFC))
        put(f"fcb{l}", _fm(inp["f_conv_b"][l], NFC))
    put("bk", _fm(inp["b_kv"][:256], 2))
    put("kn", inp["k_norm"][p % 64])
    put("qn", inp["q_norm"][0][p % 64])
    bq = inp["b_q"][0]
    cols = []
    for pi in range(2):
        for i in range(4):
            g = 2 * pi + p // 64
            cols.append(bq[(g * 4 + i) * 64 + p % 64])
    put("bq", np.stack(cols, axis=1))
    half = 32
    inv_freq = (np.float32(10000.0) ** (-np.arange(half, dtype=np.float32) / np.float32(half))).astype(np.float32)
    put("invf", inv_freq[p % 32])
    put("sgn", np.where((p % 64) < 32, -1.0, 1.0))
    sk = inp["sinks"][0]
    cols = []
    for pi in range(2):
        for i in range(4):
            cols.append(sk[(2 * pi + p // 64) * 4 + i])
    put("sink", np.stack(cols, axis=1))
    put("bo", _fm(inp["b_o"][0], 8))
    put("bv", np.tile(inp["b_kv"][256:512][None, :], (128, 1)))

    cm = np.zeros((128, NCM), np.float32)
    cm[:, 0:128] = np.eye(128)
    cm[:, 128:256] = 1.0
    cm[:, 256:384] = (p[:, None] // 64 == p[None, :] // 64)
    partner = np.where((p % 64) < 32, p + 32, p - 32)
    pm = np.zeros((128, 128), np.float32)
    pm[partner, p] = 1.0
    cm[:, 384:512] = pm
    kk = p[:, None]
    qq = p[None, :]
    mcur = np.where(kk <= qq, 0.0, -30000.0)
    mprev = np.where(kk > qq, 0.0, -30000.0)
    cm[:, 512:1024] = np.tile(mcur, (1, 4))
    cm[:, 1024:1152] = np.where(kk <= qq, 1.0, 0.0)
    cm[:, 1152:1280] = np.where(kk > qq, 1.0, 0.0)
    for h in range(32):
        for grp in range(3):
            cm[32 * grp + h, 1536 + h] = 1.0
    cmb = cm.astype(ml_dtypes.bfloat16)

    cf = np.zeros((128, NCF), np.float32)
    cf[:, 0:128] = np.eye(128)
    cf[127, 128:256] = 1.0
    return cst, cmb, cf


class Op:
    __slots__ = ("eng", "fn", "deps", "signal", "count", "semkey", "inc", "idx", "grp")


class Builder:
    ENGS = ["pe", "act", "dve", "pool", "sp"]

    def __init__(self):
        self.nc = bass.Bass("TRN2", target_bir_lowering=False)
        self.ops = {e: [] for e in self.ENGS}
        self.all_ops = []
        self.recs = {}
        self.dma_count = {}
        self.stack = ExitStack()
        self.rowbytes = {}

    def sb(self, name, shape, dt):
        name = "sb_" + name
        t = self.stack.enter_context(self.nc.sbuf_tensor(name, list(shape), dt))
        self.rowbytes[name] = int(np.prod(shape[1:])) * DT_SIZE[dt]
        return t

    def psum(self, name, shape, dt):
        name = "ps_" + name
        t = self.stack.enter_context(self.nc.psum_tensor(name, list(shape), dt))
        self.rowbytes[name] = int(np.prod(shape[1:])) * DT_SIZE[dt]
        return t

    def rect(self, ap):
        name = ap.tensor.name
        if name not in self.rowbytes:
            return None
        rb = self.rowbytes[name]
        esz = DT_SIZE[ap.dtype]
        offb = int(ap.offset) * esz
        p0 = offb // rb
        b0 = offb % rb
        dims = ap.ap
        npart = dims[0][1]
        ext = 0
        for st, n in dims[1:]:
            ext += (n - 1) * abs(st)
        b1 = b0 + (ext + 1) * esz
        ivs = None
        if name.startswith("ps_"):
            b0 = (b0 // 2048) * 2048
            b1 = ((b1 + 2047) // 2048) * 2048
        else:
            fd = [(abs(st), n) for st, n in dims[1:] if n > 1]
            if all(st > 0 for st, n in fd):
                if not fd:
                    ivs = ((b0, b0 + esz),)
                elif fd[-1][0] == 1:
                    cnt = 1
                    for st, n in fd[:-1]:
                        cnt *= n
                    if cnt <= 64:
                        starts = [0]
                        for st, n in fd[:-1]:
                            starts = [s0 + k * st for s0 in starts for k in range(n)]
                        run = fd[-1][1] * esz
                        raw = sorted((b0 + s0 * esz, b0 + s0 * esz + run) for s0 in starts)
                        merged = [list(raw[0])]
                        for lo_, hi_ in raw[1:]:
                            if lo_ <= merged[-1][1]:
                                merged[-1][1] = max(merged[-1][1], hi_)
                            else:
                                merged.append([lo_, hi_])
                        ivs = tuple((a, b_) for a, b_ in merged)
        return (name, p0, p0 + npart, b0, b1, ivs)

    @staticmethod
    def _ivs_overlap(a, b):
        i = j = 0
        while i < len(a) and j < len(b):
            if a[i][0] < b[j][1] and b[j][0] < a[i][1]:
                return True
            if a[i][1] <= b[j][1]:
                i += 1
            else:
                j += 1
        return False

    def _deps_for(self, op, reads, writes):
        deps = set()
        for is_w, aps in ((False, reads), (True, writes)):
            for ap in aps:
                r = self.rect(ap)
                if r is None:
                    continue
                name, p0, p1, b0, b1, ivs = r
                is_ps = name.startswith("ps_")
                lst = self.recs.setdefault(name, [])
                keep = []
                for rec in lst:
                    ov = not (rec[1] <= p0 or p1 <= rec[0] or rec[3] <= b0 or b1 <= rec[2])
                    if ov and ivs is not None and rec[6] is not None:
                        ov = self._ivs_overlap(ivs, rec[6])
                    if ov and (is_w or rec[4]) and rec[5] is not op:
                        deps.add(rec[5])
                    box_in = p0 <= rec[0] and rec[1] <= p1 and b0 <= rec[2] and rec[3] <= b1
                    if is_ps:
                        covered = is_w and box_in
                    else:
                        covered = is_w and box_in and ivs is not None and (len(ivs) == 1 or ivs == rec[6])
                    same_read = (not is_w) and (not rec[4]) and rec[5].eng == op.eng and rec[5].semkey == op.semkey \
                        and rec[0] == p0 and rec[1] == p1 and rec[2] == b0 and rec[3] == b1 and rec[6] == ivs
                    if not covered and not same_read:
                        keep.append(rec)
                keep.append([p0, p1, b0, b1, is_w, op, ivs])
                self.recs[name] = keep
        return deps

    def add(self, eng, fn, reads, writes, dma_sem=None):
        op = Op()
        op.eng = eng
        op.fn = fn
        op.signal = dma_sem is not None
        op.semkey = dma_sem if dma_sem is not None else eng
        op.inc = 16 if dma_sem is not None else 1
        op.count = None
        op.grp = None
        op.idx = len(self.all_ops)
        deps = self._deps_for(op, [a for a in reads if a is not None], [a for a in writes if a is not None])
        op.deps = []
        for d in deps:
            if d.semkey == eng and dma_sem is None and d.inc == 1:
                if eng == "pe" or not SAME_ENGINE_SYNC:
                    continue
            op.deps.append(d)
            d.signal = True
        self.ops[eng].append(op)
        self.all_ops.append(op)
        return op

    def mm(self, out, lhsT, rhs, start=True, stop=True):
        return self.add("pe", lambda e: e.matmul(out, lhsT, rhs, start=start, stop=stop),
                        [lhsT, rhs] + ([] if start else [out]), [out])

    def tr(self, out, in_, ident):
        return self.add("pe", lambda e: e.transpose(out, in_, ident), [in_, ident], [out])

    def act(self, out, in_, func, bias=None, scale=1.0):
        rd = [in_]
        kw = {}
        if bias is not None:
            kw["bias"] = bias
            if not isinstance(bias, (int, float)):
                rd.append(bias)
        if not isinstance(scale, (int, float)):
            rd.append(scale)
        kw["scale"] = scale
        return self.add("act", lambda e: e.activation(out, in_, func, **kw), rd, [out])

    def tt(self, out, in0, in1, op, eng="dve"):
        return self.add(eng, lambda e: e.tensor_tensor(out, in0, in1, op), [in0, in1], [out])

    def ts(self, out, in0, s1, s2, op0, op1=None, eng="dve"):
        rd = [in0] + [s for s in (s1, s2) if s is not None and not isinstance(s, (int, float))]
        if op1 is None:
            return self.add(eng, lambda e: e.tensor_scalar(out, in0, s1, None, op0), rd, [out])
        return self.add(eng, lambda e: e.tensor_scalar(out, in0, s1, s2, op0, op1), rd, [out])

    def stt(self, out, in0, scalar, in1, op0, op1):
        rd = [in0, in1] + ([] if isinstance(scalar, (int, float)) else [scalar])
        return self.add("dve", lambda e: e.scalar_tensor_tensor(out, in0, scalar, in1, op0, op1), rd, [out])

    def copy(self, out, in_, eng="dve"):
        return self.add(eng, lambda e: e.tensor_copy(out, in_), [in_], [out])

    def memset(self, ap, val, eng="dve"):
        return self.add(eng, lambda e: e.memset(ap, val), [], [ap])

    def dma(self, queue, out, in_, sem):
        return self.add(queue, lambda e: e.dma_start(out=out, in_=in_), [in_], [out], dma_sem=sem)

    def emit(self, final_waits):
        nc = self.nc
        cnt = {}
        for op in self.all_ops:
            if op.signal:
                cnt[op.semkey] = cnt.get(op.semkey, 0) + op.inc
                op.count = cnt[op.semkey]
        semkeys = sorted(cnt.keys())
        sems = {k: self.stack.enter_context(nc.semaphore("s_" + k)) for k in semkeys}
        self.nsems = len(sems)
        block = self.stack.enter_context(nc.Block())
        stats = {}

        def run(eng, e):
            waited = {}
            nw = 0
            for op in self.ops[eng]:
                need = {}
                for d in op.deps:
                    dc = (d.grp or d).count
                    if need.get(d.semkey, 0) < dc:
                        need[d.semkey] = dc
                for k, v in need.items():
                    if waited.get(k, 0) < v:
                        e.wait_ge(sems[k], v)
                        waited[k] = v
                        nw += 1
                ins = op.fn(e)
                if op.signal:
                    ins.then_inc(sems[op.semkey], op.inc)
            if eng == "sp":
                for k in final_waits:
                    if k in cnt:
                        e.wait_ge(sems[k], cnt[k])
            stats[eng] = (len(self.ops[eng]), nw)

        @block.tensor
        def _(e):
            run("pe", e)

        @block.scalar
        def _(e):
            run("act", e)

        @block.vector
        def _(e):
            run("dve", e)

        @block.gpsimd
        def _(e):
            run("pool", e)

        @block.sync
        def _(e):
            run("sp", e)

        self.stats = stats


def build(stage=99, dumps=()):
    K = Builder()
    nc = K.nc
    dram = {}

    def din(name, shape, dt=F32):
        dram[name] = nc.dram_tensor(name, list(shape), dt, kind="ExternalInput").ap()
        return dram[name]

    x_d = din("x", [T, D])
    pos_d = din("pos", [1, T], I32)
    cst_d = din("cst", [128, NCST])
    cm_d = din("cm", [128, NCM], BF16)
    cf_d = din("cf", [128, NCF])
    w_inproj = din("a_in_proj", [D, 6176])
    w_outproj = din("a_out_proj", [SSM_INNER, D])
    w_kv = din("w_kv", [D, 512])
    w_q = din("w_q", [D, D])
    w_o = din("w_o", [D, D])
    w_fin = [din(f"f_w_in{l}", [D, 2 * FFN]) for l in range(2)]
    w_fdn = [din(f"f_w_down{l}", [FFN, D]) for l in range(2)]
    out_d = nc.dram_tensor("out", [T, D], F32, kind="ExternalOutput").ap()
    dump_d = {}

    cst = K.sb("cst", [128, NCST], F32)
    cm = K.sb("cm", [128, NCM], BF16)
    cf = K.sb("cf", [128, NCF], F32)
    xT = K.sb("xT", [128, 8, TB], F32)
    kT = K.sb("kT", [128, 2, T], BF16)
    vtok = K.sb("vtok", [128, 16, 256], BF16)
    prevF = K.sb("prevF", [128, 2048], F32)
    prevB = K.sb("prevB", [128, 2048], BF16)
    halo_m = K.sb("halo_m", [128, 32, 4], F32)
    halo_f = K.sb("halo_f", [128, 2, NFC, 2], F32)
    wring = K.sb("wring", [128, NSLOT, SLOT_ELEMS], BF16)
    zsT = K.sb("zsT", [128, 16, TB], BF16)
    xcT = K.sb("xcT", [128, 16, TB], BF16)
    BT = K.sb("BT", [128, 8, TB], BF16)
    CT = K.sb("CT", [128, 8, TB], BF16)
    hT = K.sb("hT", [128, 8, TB], BF16)
    xdb = K.sb("xdb", [128, 2, 2, 2048], BF16)
    Ssb = K.sb("Ssb", [128, TB], F32)
    tokS = K.sb("tokS", [128, 4, 96], F32)
    nacum = K.sb("nacum", [128, 4, 32], F32)
    cdec = K.sb("cdec", [128, 2, 32], F32)
    CBb = K.sb("CBb", [128, 2, 384], BF16)
    rawb = K.sb("rawb", [128, 3, 516], F32)
    accb = K.sb("accb", [128, 2, TB], F32)
    sqb = K.sb("sqb", [128, 2, TB], BF16)
    lnb = K.sb("lnb", [128, TB], F32)
    rstd = K.sb("rstd", [128, TB], F32)
    Eb = K.sb("Eb", [128, 4, 384], BF16)
    Erow = K.sb("Erow", [128, 4, 256], BF16)
    Wt = K.sb("Wt", [128, 4, 384], BF16)
    Cs = K.sb("Cs", [128, 4, 256], BF16)
    ytmp = K.sb("ytmp", [128, 2, 256], F32)
    stmp = K.sb("stmp", [128, 256], F32)
    def f32v(t, dt=F32):
        return t[:].bitcast(dt).rearrange("p (a two) n -> p a (two n)", two=2)
    BTf, CTf, hTf = f32v(BT), f32v(CT), f32v(hT)
    angb = BTf[:, 0:2, :]
    cosT = BTf[:, 2, :]
    sinT = BTf[:, 3, :]
    kfb = CTf[:, 0, :]
    qraw = CTf[:, 2:4, :]
    t1b = hTf[:, 0:2, :]
    dtot = hTf[:, 2, :]
    qnb = K.sb("qnb", [128, 2, TB], BF16)
    xtk = K.sb("xtk", [128, 2, TB], BF16)
    sgb = xtk
    ibuf = K.sb("ibuf", [128, TB], I32)
    posi = ibuf[:]
    kib = ibuf[:]
    AH = K.sb("AH", [128, TB], BF16)
    PTb = rawb[:].bitcast(BF16).rearrange("p a n -> p (a n)")[:, 0:4 * TB].rearrange("p (a n) -> p a n", a=4)
    esink = K.sb("esink", [128, 8], F32)
    avec = K.sb("avec", [128, 1], F32)
    ps = K.psum("ps", [128, 8, 512], F32)

    Btok = hT
    xin = xdb
    hkvT = zsT
    qT = xcT
    attT = xcT

    def C(name, j=None, n=None):
        o, w = CST[name]
        if j is None:
            return cst[:, o:o + w]
        return cst[:, o + j:o + j + (n or 1)]

    def CMv(name):
        o, w = CM[name]
        return cm[:, o:o + w]

    ident = CMv("ident")
    ones = CMv("ones")
    identf = cf[:, 0:128]
    sel127 = cf[:, 128:256]

    bank_ctr = [0]

    def bank():
        b = bank_ctr[0] % 8
        bank_ctr[0] += 1
        return b

    def dump(name, ap):
        if name in dumps:
            dump_d[name] = nc.dram_tensor("dbg_" + name, list(ap.shape), ap.dtype, kind="ExternalOutput").ap()
            K.dma("sp", dump_d[name], ap, "dbg_" + name)

    wsched = []
    wissued = [0]
    wviews = {}

    def wplan(key, parts, shape):
        wsched.append((key, parts, shape))

    def wview(slot, shape):
        n = int(np.prod(shape[1:]))
        v = wring[:, slot, 0:n]
        if len(shape) == 3:
            v = v.rearrange("p (a b) -> p a b", a=shape[1])
        return v

    def wissue_upto(i):
        while wissued[0] <= min(i, len(wsched) - 1):
            j = wissued[0]
            key, parts, shape = wsched[j]
            slot = j % NSLOT
            v = wview(slot, shape)
            wviews[key] = v
            tile_ops = [K.dma("pool", fn(v), src, f"w{slot}") for fn, src in parts]
            for o_ in tile_ops:
                o_.grp = tile_ops[-1]
            wissued[0] += 1

    wpos = {}

    def wget(key):
        i = wpos[key]
        wissue_upto(i + NSLOT - 1)
        return wviews[key]

    def kc_view(w, c0, n, kcs=8, r0=0):
        return w[r0:r0 + kcs * 128, c0:c0 + n].rearrange("(kc p) n -> p kc n", p=128)

    for b in range(NBLK):
        wplan(("inp", b, 12), [((lambda v, r=r: v[:, :, r * 32:(r + 1) * 32]), kc_view(w_inproj, 6144, 32)) for r in range(3)],
              [128, 8, 96])
        for t in INP_ORDER:
            wplan(("inp", b, t), [(lambda v: v, kc_view(w_inproj, t * 512, 512))], [128, 8, 512])
        for t in range(4):
            wplan(("outp", b, t), [(lambda v: v, kc_view(w_outproj, t * 256, 256, kcs=16))], [128, 16, 256])
        for l in range(2):
            if l == 1:
                wplan(("wkv", b), [(lambda v: v, kc_view(w_kv, 0, 512))], [128, 8, 512])
                for pi in range(2):
                    parts = []
                    for i in range(4):
                        for half in range(2):
                            col0 = ((2 * pi + half) * 4 + i) * 64
                            parts.append(((lambda v, i=i, half=half: v[:, :, i * 128 + half * 64:i * 128 + half * 64 + 64]),
                                          kc_view(w_q, col0, 64)))
                    wplan(("wq", b, pi), parts, [128, 8, 512])
                for t in range(2):
                    parts = []
                    for half in range(2):
                        for pi in range(2):
                            r0 = ((2 * pi + half) * 4) * 64
                            src = w_o[r0:r0 + 256, t * 512:(t + 1) * 512].rearrange("(i d) n -> d i n", i=4)
                            parts.append(((lambda v, half=half, pi=pi: v[half * 64:(half + 1) * 64, pi * 4:(pi + 1) * 4, :]), src))
                    wplan(("wo", b, t), parts, [128, 8, 512])
            for t in range(11):
                parts = [((lambda v: v[:, :, 0:256]), kc_view(w_fin[l], t * 256, 256)),
                         ((lambda v: v[:, :, 256:512]), kc_view(w_fin[l], FFN + t * 256, 256))]
                wplan(("fin", b, l, t), parts, [128, 8, 512])
            for cg in range(2):
                for part, (k0, nk) in enumerate(FDN_PARTS):
                    wplan(("fdn", b, l, cg, part), [(lambda v: v, kc_view(w_fdn[l], cg * 512, 512, kcs=nk, r0=k0 * 128))],
                          [128, nk, 512])
    for i, (key, _, _) in enumerate(wsched):
        wpos[key] = i

    K.dma("sp", cst[:], cst_d, "c0")
    K.dma("sp", cm[:], cm_d, "c1")
    K.dma("sp", cf[:], cf_d, "c2")
    K.memset(halo_m[:], 0.0)
    K.memset(halo_f[:], 0.0)
    K.memset(prevF[:], 0.0)
    K.memset(prevB[:], 0.0)
    K.memset(Ssb[:], 0.0)
    K.memset(AH[:], 0.0)
    K.act(avec[0:96, :], C("alog3")[0:96, :], AF.Exp)
    K.ts(avec[0:96, :], avec[0:96, :], -1.0, None, OP.mult)
    K.act(esink[:], C("sink"), AF.Exp)

    def rms_stats(src_fn, nch, lhs, scale):
        bk = bank()
        for c in range(nch):
            if c % 2 == 0:
                K.act(sqb[:, 0, :], src_fn(c), AF.Square)
            else:
                K.tt(sqb[:, 1, :], src_fn(c), src_fn(c), OP.mult)
            K.mm(ps[:, bk, :], lhs, sqb[:, c % 2, :], start=(c == 0), stop=(c == nch - 1))
        K.act(lnb[:], ps[:, bk, :], AF.Ln, bias=epsb[:, 0:1], scale=scale)
        K.act(rstd[:], lnb[:], AF.Exp, scale=-0.5)

    epsb = K.sb("epsb", [128, 1], F32)
    K.memset(epsb[:], EPS)
    onesf = K.sb("onesf", [128, 256], F32)
    K.memset(onesf[:], 1.0)

    def conv_taps(dst_acc, src_ps, raw, ntap, wname_fn, bname):
        K.act(dst_acc, src_ps, AF.Identity, bias=bname, scale=wname_fn(ntap - 1))
        for k in range(ntap - 2, -1, -1):
            K.stt(dst_acc, raw[:, k:k + TB], wname_fn(k), dst_acc, OP.mult, OP.add)

    def ffn(l, b, mid_hook=None):
        rms_stats(lambda c: xT[:, c, :], 8, ones, 1.0 / D)
        for c in range(8):
            K.stt(hT[:, c, :], xT[:, c, :], C(f"f_norm{l}", c), rstd[:], OP.mult, OP.mult)
        uT = zsT
        def u(j):
            return zsT[:, j, :] if j < 16 else xcT[:, j - 16, :]
        pend = []
        for t in range(11):
            w = wget(("fin", b, l, t))
            for jj in range(2):
                j = 2 * t + jj
                bg_, bv_ = bank(), bank()
                for kc in range(8):
                    K.mm(ps[:, bg_, :], w[:, kc, jj * 128:(jj + 1) * 128], hT[:, kc, :], start=(kc == 0), stop=(kc == 7))
                for kc in range(8):
                    K.mm(ps[:, bv_, :], w[:, kc, 256 + jj * 128:256 + (jj + 1) * 128], hT[:, kc, :], start=(kc == 0), stop=(kc == 7))
                r = j % 3
                K.act(rawb[:, r, 0:2], halo_f[:, l, j, :], AF.Copy)
                K.act(rawb[:, r, 2:2 + TB], ps[:, bg_, :], AF.Copy)
                K.act(halo_f[:, l, j, :], ps[:, bg_, TB - 2:TB], AF.Copy)
                acc = accb[:, j % 2, :]
                conv_taps(acc, ps[:, bg_, :], rawb[:, r, :], 3, lambda k: C(f"fcw{l}_{k}", j), C(f"fcb{l}", j))
                for f in pend:
                    f()
                pend.clear()

                def fin(j=j, acc=acc, bv_=bv_):
                    K.act(sgb[:, j % 2, :], acc, AF.Silu)
                    K.tt(u(j), sgb[:, j % 2, :], ps[:, bv_, :], OP.mult)
                pend.append(fin)
        for f in pend:
            f()
        if mid_hook is not None:
            mid_hook()
        for cg in range(2):
            banks = [bank() for _ in range(4)]
            for part, (k0, nk) in enumerate(FDN_PARTS):
                w = wget(("fdn", b, l, cg, part))
                for jj in range(4):
                    for kk in range(nk):
                        kc = k0 + kk
                        K.mm(ps[:, banks[jj], :], w[:, kk, jj * 128:(jj + 1) * 128], u(kc), start=(kc == 0), stop=(kc == 21))
            for jj in range(4):
                oc = 4 * cg + jj
                K.tt(xT[:, oc, :], xT[:, oc, :], ps[:, banks[jj], :], OP.add)

    for b in range(NBLK):
        t0 = b * TB
        def rope_tables():
            K.dma("sp", posi, pos_d[0:1, t0:t0 + TB].partition_broadcast(128), "posd")
            posf = t1b[:, 0, :]
            K.copy(posf, posi)
            TWO_PI = 2.0 * math.pi
            C1 = 6.28125
            C2 = TWO_PI - C1
            for which, shift, dst in ((0, 0.0, sinT), (1, math.pi / 2.0, cosT)):
                ang = angb[:, which, :]
                K.ts(ang, posf, C("invf"), shift, OP.mult, OP.add)
                K.ts(kfb, ang, 1.0 / TWO_PI, None, OP.mult)
                K.copy(kib, kfb)
                K.copy(kfb, kib)
                K.stt(ang, kfb, -C1, ang, OP.mult, OP.add)
                K.stt(ang, kfb, -C2, ang, OP.mult, OP.add)
                K.ts(kfb, ang, math.pi, -TWO_PI, OP.is_gt, OP.mult)
                K.tt(ang, ang, kfb, OP.add)
                K.ts(kfb, ang, -math.pi, TWO_PI, OP.is_lt, OP.mult)
                K.tt(ang, ang, kfb, OP.add)
                K.ts(ang, ang, math.pi, -math.pi, OP.min, OP.max)
                K.act(dst[:], ang, AF.Sin)
            K.ts(sinT, sinT, C("sgn"), None, OP.mult)

        xin_v = xin[:].bitcast(F32).rearrange("p a b (c n) -> p (a b c) n", n=1024)
        if b == 0:
            for tl in range(4):
                K.dma("sp", xin_v[:, tl, :], x_d[t0 + tl * 128:t0 + (tl + 1) * 128, :], f"xin{tl}")
        for c in range(8):
            bk = bank()
            for tl in range(4):
                K.tr(ps[:, bk, tl * 128:(tl + 1) * 128], xin_v[:, tl, c * 128:(c + 1) * 128], identf)
            K.act(xT[:, c, :], ps[:, bk, :], AF.Copy)
        if b == 0:
            dump("xT0", xT[:])
        if stage <= 0:
            continue
        rms_stats(lambda c: xT[:, c, :], 8, ones, 1.0 / D)
        for c in range(8):
            K.stt(hT[:, c, :], xT[:, c, :], C("a_norm", c), rstd[:], OP.mult, OP.mult)
        if b == 0:
            dump("hT0", hT[:])
        w = wget(("inp", b, 12))
        bk = bank()
        for kc in range(8):
            K.mm(ps[0:96, bk, :], w[:, kc, :], hT[:, kc, :], start=(kc == 0), stop=(kc == 7))
        K.act(lnb[0:96, :], ps[0:96, bk, :], AF.Exp, bias=C("dtb3")[0:96, :])
        dt3 = accb[0:96, 0, :]
        K.act(dt3, lnb[0:96, :], AF.Ln, bias=1.0)
        a3 = accb[0:96, 1, :]
        K.ts(a3, dt3, avec[0:96, :], None, OP.mult)
        ac3 = lnb[0:96, :]
        for ck in range(2):
            K.add("dve", (lambda e, ck=ck: e.tensor_tensor_scan(ac3[:, ck * 256:(ck + 1) * 256], onesf[0:96, 0:256],
                                                              a3[:, ck * 256:(ck + 1) * 256], 0.0, OP.mult, OP.add)),
                  [onesf[0:96, 0:256], a3[:, ck * 256:(ck + 1) * 256]], [ac3[:, ck * 256:(ck + 1) * 256]])
        K.copy(Ssb[0:32, :], dt3[0:32, :])
        K.copy(Ssb[64:96, :], ac3[64:96, :])
        for ck in range(2):
            sl = slice(ck * 256, (ck + 1) * 256)
            K.act(Ssb[32:64, sl], ac3[32:64, sl], AF.Exp, bias=ac3[32:64, ck * 256 + 255:ck * 256 + 256], scale=-1.0)
        K.tt(Ssb[32:64, :], Ssb[32:64, :], dt3[32:64, :], OP.mult)
        for g0 in (0, 32, 64):
            K.copy(AH[g0:g0 + 32, :], ac3[g0:g0 + 32, :])
        for g0 in (32, 64):
            K.tt(accb[g0:g0 + 32, 0, :], ac3[g0:g0 + 32, :], AH[g0:g0 + 32, :], OP.subtract)
            K.copy(AH[g0:g0 + 32, :], accb[g0:g0 + 32, 0, :])
        K.tt(accb[64:96, 1, :], accb[64:96, 0, :], AH[64:96, :], OP.subtract)
        K.copy(AH[64:96, :], accb[64:96, 1, :])
        def tok_decay():
            for tl in range(4):
                bk = bank()
                K.tr(ps[:, bk, 0:96], Ssb[0:96, tl * 128:(tl + 1) * 128], identf[0:96, 0:96])
                K.copy(tokS[:, tl, :], ps[:, bk, 0:96])
                K.ts(nacum[:, tl, :], tokS[:, tl, 64:96], -1.0, None, OP.mult)

        def chunk_decay():
            for ck in range(2):
                bk = bank()
                K.mm(ps[:, bk, 0:32], sel127, tokS[:, 2 * ck + 1, 64:96])
                K.act(cdec[:, ck, :], ps[:, bk, 0:32], AF.Exp)

        def xd_tiles(tl):
            for c4 in range(4):
                bk = bank()
                pb = ps[:, bk, :].bitcast(BF16)
                for q in range(4):
                    K.tr(pb[:, q * 128:(q + 1) * 128], xcT[:, c4 * 4 + q, tl * 128:(tl + 1) * 128], ident)
                xk = xtk[:, c4 % 2, :]
                K.copy(xk, pb[:, 0:512])
                src = xk.rearrange("p (h d) -> p h d", d=64)
                for which, col0 in ((0, 0), (1, 32)):
                    dst = xdb[:, which, tl % 2, c4 * 512:(c4 + 1) * 512].rearrange("p (h d) -> p h d", d=64)
                    sc = tokS[:, tl, col0 + c4 * 8:col0 + c4 * 8 + 8].unsqueeze(2).to_broadcast([128, 8, 64])
                    K.tt(dst, src, sc, OP.mult)

        pend = []
        for t in INP_ORDER:
            w = wget(("inp", b, t))
            for jj in range(4):
                ch = 4 * t + jj
                bk = bank()
                for kc in range(8):
                    K.mm(ps[:, bk, :], w[:, kc, jj * 128:(jj + 1) * 128], hT[:, kc, :], start=(kc == 0), stop=(kc == 7))
                if ch < 16:
                    K.act(zsT[:, ch, :], ps[:, bk, :], AF.Silu)
                else:
                    j = ch - 16
                    r = j % 3
                    K.copy(rawb[:, r, 0:3], halo_m[:, j, 0:3], eng="pool")
                    K.act(rawb[:, r, 3:3 + TB], ps[:, bk, :], AF.Copy)
                    K.copy(halo_m[:, j, 0:3], rawb[:, r, TB:TB + 3], eng="pool")
                    acc = accb[:, j % 2, :]
                    conv_taps(acc, ps[:, bk, :], rawb[:, r, :], 4, lambda k: C(f"cw{k}", j), C("cb", j))
                    if j < 16:
                        dst = xcT[:, j, :]
                    elif j < 24:
                        dst = BT[:, j - 16, :]
                    else:
                        dst = CT[:, j - 24, :]
                    for f in pend:
                        f()
                    pend.clear()
                    pend.append(lambda dst=dst, acc=acc: K.act(dst, acc, AF.Silu))
            if t == INP_ORDER[1]:
                tok_decay()
            if t == 9:
                xd_tiles(0)
                xd_tiles(1)
        for f in pend:
            f()
        pend.clear()
        if b == 0:
            dump("S0", Ssb[:])
            dump("xcT0", xcT[:])
            dump("zsT0", zsT[:])
            dump("BT0", BT[:])
        if stage <= 1:
            continue
        chunk_decay()
        Btok_v = Btok[:].rearrange("p a b -> p (a b)").rearrange("p (t n) -> p t n", t=4)
        for tl in range(4):
            for g4 in range(2):
                bk = bank()
                pb = ps[:, bk, :].bitcast(BF16)
                for q in range(4):
                    K.tr(pb[:, q * 128:(q + 1) * 128], BT[:, g4 * 4 + q, tl * 128:(tl + 1) * 128], ident)
                K.copy(Btok_v[:, tl, g4 * 512:(g4 + 1) * 512], pb[:, 0:512])

        for ck in range(2):
            tl0, tl1 = 2 * ck, 2 * ck + 1
            c0 = ck * 256
            if ck > 0:
                xd_tiles(tl0)
                xd_tiles(tl1)
            items = [(g, hh) for g in range(8) for hh in range(4)]

            def s1(i):
                g, hh = items[i]
                h = 4 * g + hh
                gcb = g if i == 0 else (g + 1 if (hh == 3 and i + 1 < len(items)) else None)
                if gcb is not None:
                    bcb = gcb % 2
                    K.mm(ps[:, bcb, 0:256], BT[:, gcb, c0:c0 + 128], CT[:, gcb, c0:c0 + 256])
                    K.mm(ps[:, bcb, 256:384], BT[:, gcb, c0 + 128:c0 + 256], CT[:, gcb, c0 + 128:c0 + 256])
                oh = cm[:, 1536 + h:1537 + h].to_broadcast([128, 128])
                pa4 = ps[:, 5 + i % 3, :].rearrange("p (t r n) -> p t r n", t=2, r=2)
                rhs4 = AH[:, c0:c0 + 256].rearrange("p (t n) -> p t n", t=2).unsqueeze(2).to_broadcast([128, 2, 2, 128])
                K.mm(pa4, oh, rhs4, start=True, stop=False)
                K.mm(ps[:, 5 + i % 3, 128:256], ident, CMv("mcur")[:, 0:128], start=False, stop=False)
                K.mm(ps[:, 5 + i % 3, 384:512], ident, CMv("mcur")[:, 0:128], start=False, stop=True)

            def s2(i):
                g, hh = items[i]
                h = 4 * g + hh
                r = i % 4
                ba = 5 + i % 3
                bcb = g % 2
                if i == 0:
                    K.act(CBb[:, g % 2, :], ps[:, bcb, 0:384], AF.Copy)
                pa4 = ps[:, ba, :].rearrange("p (t r n) -> p t r n", t=2, r=2)
                K.act(Erow[:, r, :].rearrange("p (t n) -> p t n", t=2), pa4[:, :, 0, :], AF.Exp)
                K.act(Eb[:, r, 0:256], ps[:, ba, 128:384], AF.Exp, bias=nacum[:, tl0, h:h + 1])
                K.act(Eb[:, r, 256:384], ps[:, ba, 384:512], AF.Exp, bias=nacum[:, tl1, h:h + 1])
                if hh == 3 and i + 1 < len(items):
                    K.act(CBb[:, (g + 1) % 2, :], ps[:, (g + 1) % 2, 0:384], AF.Copy)
                K.tt(Wt[:, r, :], CBb[:, g % 2, :], Eb[:, r, :], OP.mult)
                K.tt(Cs[:, r, :], CT[:, g, c0:c0 + 256], Erow[:, r, :], OP.mult)

            def s3(i):
                g, hh = items[i]
                h = 4 * g + hh
                r = i % 4
                hp = h % 2
                by = 2 + ((h // 2) % 2)
                po = ps[hp * 64:(hp + 1) * 64, by, 0:256]
                K.mm(po, xdb[:, 0, 0, h * 64:(h + 1) * 64], Wt[:, r, 0:256], start=True, stop=False)
                K.mm(ps[hp * 64:(hp + 1) * 64, by, 128:256], xdb[:, 0, 1, h * 64:(h + 1) * 64], Wt[:, r, 256:384], start=False, stop=False)
                K.mm(po, prevB[:, h * 64:(h + 1) * 64], Cs[:, r, :], start=False, stop=True)
                if hh % 2 == 1:
                    c = h // 2
                    yt = ytmp[:, c % 2, :]
                    K.stt(yt, xcT[:, c, c0:c0 + 256], C("Dc", c), ps[:, by, 0:256], OP.mult, OP.add)
                    K.tt(zsT[:, c, c0:c0 + 256], yt, zsT[:, c, c0:c0 + 256], OP.mult)
                if hh == 3:
                    bs = 4
                    K.mm(ps[:, bs, 0:256], Btok_v[:, tl0, g * 128:(g + 1) * 128], xdb[:, 1, 0, g * 256:(g + 1) * 256], start=True, stop=False)
                    K.mm(ps[:, bs, 0:256], Btok_v[:, tl1, g * 128:(g + 1) * 128], xdb[:, 1, 1, g * 256:(g + 1) * 256], start=False, stop=True)
                    pv = prevF[:, g * 256:(g + 1) * 256]
                    K.tt(stmp[:].rearrange("p (h d) -> p h d", d=64), pv.rearrange("p (h d) -> p h d", d=64),
                         cdec[:, ck, 4 * g:4 * g + 4].unsqueeze(2).to_broadcast([128, 4, 64]), OP.mult, eng="pool")
                    K.tt(pv, stmp[:], ps[:, bs, 0:256], OP.add)
                    K.copy(prevB[:, g * 256:(g + 1) * 256], pv)

            NI = len(items)
            for i in range(NI + 2):
                if i < NI:
                    s1(i)
                if 0 <= i - 1 < NI:
                    s2(i - 1)
                if 0 <= i - 2 < NI:
                    s3(i - 2)
        if b == 0:
            dump("yg0", zsT[:])
        if b + 1 < NBLK and stage > 4:
            for tl in range(4):
                K.dma("sp", xin_v[:, tl, :], x_d[t0 + TB + tl * 128:t0 + TB + (tl + 1) * 128, :], f"xin{tl}")
        gl = [lnb[:], accb[:, 0, :]]
        gr = [rstd[:], accb[:, 1, :]]
        gpend = []
        for g in range(8):
            bkg = bank()
            for c in range(2):
                K.act(sqb[:, c, :], zsT[:, 2 * g + c, :], AF.Square)
                K.mm(ps[:, bkg, :], ones, sqb[:, c, :], start=(c == 0), stop=(c == 1))
            for f in gpend:
                f()
            gpend.clear()

            def gapply(g=g, bkg=bkg):
                K.act(gl[g % 2], ps[:, bkg, :], AF.Ln, bias=epsb[:, 0:1], scale=1.0 / 256)
                K.act(gr[g % 2], gl[g % 2], AF.Exp, scale=-0.5)
                for c in (2 * g, 2 * g + 1):
                    K.stt(zsT[:, c, :], zsT[:, c, :], C("gn", c), gr[g % 2], OP.mult, OP.mult)
            gpend.append(gapply)
        for f in gpend:
            f()
        gpend.clear()
        for t in range(4):
            w = wget(("outp", b, t))
            for jj in range(2):
                bk = bank()
                for kc in range(16):
                    K.mm(ps[:, bk, :], w[:, kc, jj * 128:(jj + 1) * 128], zsT[:, kc, :], start=(kc == 0), stop=(kc == 15))
                oc = 2 * t + jj
                K.tt(xT[:, oc, :], xT[:, oc, :], ps[:, bk, :], OP.add)
        if b == 0:
            dump("x1T0", xT[:])
        if stage <= 2:
            continue
        ffn(0, b, mid_hook=(rope_tables if stage > 3 else None))
        if b == 0:
            dump("x2T0", xT[:])
        if stage <= 3:
            continue
        rms_stats(lambda c: xT[:, c, :], 8, ones, 1.0 / D)
        for c in range(8):
            K.stt(hkvT[:, c, :], xT[:, c, :], C("kv_norm", c), rstd[:], OP.mult, OP.mult)
        for c in range(8):
            K.stt(hkvT[:, 8 + c, :], xT[:, c, :], C("b_norm", c), rstd[:], OP.mult, OP.mult)
        lnbs = [lnb[:], accb[:, 0, :], ytmp[:].rearrange("p a n -> p (a n)")]
        rstds = [rstd[:], accb[:, 1, :], Ssb[:]]
        qraws = [qraw[:, 0, :], qraw[:, 1, :], kfb]
        sqs = [sqb[:, 0, :], sqb[:, 1, :], xtk[:, 0, :]]
        qnbs = [qnb[:, 0, :], qnb[:, 1, :], xtk[:, 1, :]]
        t1s = [t1b[:, 0, :], t1b[:, 1, :], hTf[:, 3, :]]
        pa, pb = [], []

        def hnr_step(new_pair):
            for f in pb:
                f()
            pb.clear()
            for fa, fb in pa:
                fa()
                pb.append(fb)
            pa.clear()
            if new_pair is not None:
                pa.append(new_pair)

        def hnr_p1(mm_fn, bias_col, gain_col, dst, r):
            bk = bank()
            mm_fn(bk)
            K.act(qraws[r], ps[:, bk, :], AF.Identity, bias=bias_col)
            K.act(sqs[r], qraws[r], AF.Square)

            def p2a():
                b2 = bank()
                K.mm(ps[:, b2, :], CMv("bd64"), sqs[r])
                K.act(lnbs[r], ps[:, b2, :], AF.Ln, bias=epsb[:, 0:1], scale=1.0 / 64)
                K.act(rstds[r], lnbs[r], AF.Exp, scale=-0.5)
                K.stt(qnbs[r], qraws[r], gain_col, rstds[r], OP.mult, OP.mult)

            def p2b():
                b3 = bank()
                K.mm(ps[:, b3, :], CMv("perm"), qnbs[r])
                K.tt(t1s[r], qnbs[r], cosT, OP.mult)
                K.tt(qraws[r], ps[:, b3, :], sinT, OP.mult)
                K.tt(dst, t1s[r], qraws[r], OP.add)
            hnr_step((p2a, p2b))

        w = wget(("wkv", b))
        wkv_t = w
        cidx = 0
        for oc in range(2):
            def mmk(bk, oc=oc):
                for kc in range(8):
                    K.mm(ps[:, bk, :], wkv_t[:, kc, oc * 128:(oc + 1) * 128], hkvT[:, kc, :], start=(kc == 0), stop=(kc == 7))
            hnr_p1(mmk, C("bk", oc), C("kn"), kT[:, oc, t0:t0 + TB], cidx % 3)
            cidx += 1
        for tl in range(4):
            bk = bank()
            for kc in range(8):
                K.mm(ps[:, bk, 0:256], hkvT[:, kc, tl * 128:(tl + 1) * 128], w[:, kc, 256:512], start=(kc == 0), stop=(kc == 7))
            K.tt(vtok[:, 4 * b + tl, :], ps[:, bk, 0:256], C("bv"), OP.add)
        qT_v = qT[:, 0:8, :].rearrange("p (pi i) t -> p pi i t", pi=2)
        attT_v = attT[:, 8:16, :].rearrange("p (pi i) t -> p pi i t", pi=2)
        apend = []
        npend = []
        cnts = [0, 0, cidx]

        def qproj(pi, wq_t, i_list):
            for i in i_list:
                def mmq(bk, i=i, wq_t=wq_t):
                    for kc in range(8):
                        K.mm(ps[:, bk, :], wq_t[:, kc, i * 128:(i + 1) * 128], hkvT[:, 8 + kc, :], start=(kc == 0), stop=(kc == 7))
                hnr_p1(mmq, C("bq", pi * 4 + i), C("qn"), qT_v[:, pi, i, :], cnts[2] % 3)
                cnts[2] += 1

        def core(pi):
            for nl in range(4):
                n = 4 * b + nl
                qs = slice(nl * 128, (nl + 1) * 128)
                bo_, bd_ = 4 + cnts[1] % 2, 6 + cnts[1] % 2
                cnts[1] += 1
                units = []
                for half in range(2):
                    tiles = ([(n - 1, cm[:, 1152:1280])] if n > 0 else []) + [(n, cm[:, 1024:1152])]
                    for ti, (kt, msk) in enumerate(tiles):
                        units.append((half, ti, kt, msk, len(tiles)))
                for ui, (half, ti, kt, msk, ntl) in enumerate(units):
                    g = 2 * pi + half
                    P0 = half * 64
                    ucnt = cnts[0]
                    bs_ = ucnt % 3
                    K.mm(ps[:, bs_, :].rearrange("p (i q) -> p i q", i=4), kT[P0:P0 + 64, pi, kt * 128:(kt + 1) * 128],
                         qT_v[P0:P0 + 64, pi, :, qs], start=True, stop=True)
                    for f in apend:
                        f()
                    apend.clear()

                    def fin(bs_=bs_, ucnt=ucnt, P0=P0, g=g, kt=kt, ti=ti, ntl=ntl, bo_=bo_, bd_=bd_, msk=msk, last_unit=(ui == len(units) - 1), pi=pi, qs=qs, last_block=(nl == 3)):
                        pt = PTb[:, ucnt % 4, :]
                        K.act(pt, ps[:, bs_, :], AF.Exp, scale=0.125)
                        pt3 = pt.rearrange("p (i q) -> p i q", i=4)
                        K.tt(pt3, pt3, msk.unsqueeze(1).to_broadcast([128, 4, 128]), OP.mult)
                        first, last = (ti == 0), (ti == ntl - 1)
                        K.mm(ps[P0:P0 + 64, bo_, :], vtok[:, kt, g * 64:(g + 1) * 64], pt, start=first, stop=last)
                        K.mm(ps[P0:P0 + 64, bd_, :], ones[:, 0:64], pt, start=first, stop=last)
                        for fn_ in npend:
                            fn_()
                        npend.clear()
                        if last_unit:
                            def norm():
                                for i in range(4):
                                    K.ts(dtot[:, i * 128:(i + 1) * 128], ps[:, bd_, i * 128:(i + 1) * 128], esink[:, pi * 4 + i:pi * 4 + i + 1], None, OP.add)
                                if last_block:
                                    K.act(lnb[:], dtot, AF.Ln)
                                    K.act(dtot, lnb[:], AF.Exp, scale=-1.0)
                                else:
                                    K.add("dve", (lambda e: e.reciprocal(dtot, dtot)), [dtot], [dtot])
                                for i in range(4):
                                    K.tt(attT_v[:, pi, i, qs], ps[:, bo_, i * 128:(i + 1) * 128], dtot[:, i * 128:(i + 1) * 128], OP.mult)
                            npend.append(norm)
                    apend.append(fin)
                    cnts[0] += 1
            for f in apend:
                f()
            apend.clear()
            for fn_ in npend:
                fn_()
            npend.clear()

        wq0 = wget(("wq", b, 0))
        qproj(0, wq0, [0, 1, 2, 3])
        wq1 = wget(("wq", b, 1))
        qproj(1, wq1, [0, 1])
        core(0)
        qproj(1, wq1, [2, 3])
        hnr_step(None)
        hnr_step(None)
        core(1)
        if b == 0:
            dump("kT0", kT[:, :, 0:TB])
            dump("qT0", qT[:, 0:8, :])
            dump("v0", vtok[:, 0:4, :])
        if b == 0:
            dump("att0", attT[:, 8:16, :])
        for t in range(2):
            w = wget(("wo", b, t))
            banks = [bank() for _ in range(4)]
            for kc in range(8):
                for jj in range(4):
                    K.mm(ps[:, banks[jj], :], w[:, kc, jj * 128:(jj + 1) * 128], attT[:, 8 + kc, :], start=(kc == 0), stop=(kc == 7))
            for jj in range(4):
                oc = 4 * t + jj
                K.stt(xT[:, oc, :], ps[:, banks[jj], :], C("bo", oc), xT[:, oc, :], OP.add, OP.add)
        if b == 0:
            dump("x3T0", xT[:])
        if stage <= 4:
            continue
        ffn(1, b)
        xo_v = zsT[:].bitcast(F32).rearrange("p (a b) n -> p a (b n)", b=4)
        for tl in range(4):
            for c2 in range(2):
                bk = bank()
                for q in range(4):
                    c = 4 * c2 + q
                    K.tr(ps[:, bk, q * 128:(q + 1) * 128], xT[:, c, tl * 128:(tl + 1) * 128], identf)
                K.act(xo_v[:, tl, c2 * 512:(c2 + 1) * 512], ps[:, bk, :], AF.Copy)
            K.dma("sp", out_d[t0 + tl * 128:t0 + (tl + 1) * 128, :], xo_v[:, tl, :], f"out{tl}")

    finals = [f"out{tl}" for tl in range(4)] + ["dbg_" + n for n in dump_d]
    K.emit(finals)
    return K, dump_d


_CACHE = {}


def _prep_inputs(inputs):
    inp = {k: np.asarray(v) for k, v in inputs.items()}
    cst, cmb, cf = _host_consts(inp)
    shared = {
        "cst": cst, "cm": cmb, "cf": cf,
        "a_in_proj": np.ascontiguousarray(inp["a_in_proj"][0]),
        "a_out_proj": np.ascontiguousarray(inp["a_out_proj"][0]),
        "w_kv": np.ascontiguousarray(inp["w_kv"]),
        "w_q": np.ascontiguousarray(inp["w_q"][0]),
        "w_o": np.ascontiguousarray(inp["w_o"][0]),
        "f_w_in0": np.ascontiguousarray(inp["f_w_in"][0]),
        "f_w_in1": np.ascontiguousarray(inp["f_w_in"][1]),
        "f_w_down0": np.ascontiguousarray(inp["f_w_down"][0]),
        "f_w_down1": np.ascontiguousarray(inp["f_w_down"][1]),
    }
    maps = []
    for c in range(8):
        m = dict(shared)
        m["x"] = np.ascontiguousarray(inp["x"][c])
        m["pos"] = np.ascontiguousarray(inp["positions"][c].reshape(1, T).astype(np.int32))
        maps.append(m)
    return maps


def kernel(**inputs):
    maps = _prep_inputs(inputs)
    K, _ = build()
    res = run_bass_kernel_spmd(K.nc, maps, core_ids=list(range(8)))
    return np.stack([np.asarray(r["out"], dtype=np.float32) for r in res.results], axis=0)
```

```python
import math
from contextlib import ExitStack

import numpy as np
import ml_dtypes
import concourse.bass as bass
import concourse.mybir as mybir
from concourse.bass_utils import run_bass_kernel_spmd

F32 = mybir.dt.float32
BF16 = mybir.dt.bfloat16
I32 = mybir.dt.int32
AF = mybir.ActivationFunctionType
OP = mybir.AluOpType
DT_SIZE = {F32: 4, BF16: 2, I32: 4}

T = 2048
D = 1024
TB = 512
NBLK = T // TB
EPS = 1e-5
SSM_INNER = 2048
NHEAD = 32
FFN = 2816
NFC = FFN // 128
SAME_ENGINE_SYNC = True
NSLOT = 5
INP_ORDER = [4, 5, 0, 6, 7, 1, 8, 9, 2, 10, 11, 3]
FDN_PARTS = [(0, 6), (6, 6), (12, 5), (17, 5)]
SLOT_ELEMS = 8 * 512


def _cst_layout():
    lay = {}
    off = 0

    def add(name, n):
        nonlocal off
        lay[name] = (off, n)
        off += n

    for nm in ["a_norm", "f_norm0", "f_norm1", "kv_norm", "b_norm"]:
        add(nm, 8)
    for k in range(4):
        add(f"cw{k}", 32)
    add("cb", 32)
    add("dtb3", 1)
    add("alog3", 1)
    add("Dc", 16)
    add("gn", 16)
    for l in range(2):
        for k in range(3):
            add(f"fcw{l}_{k}", NFC)
        add(f"fcb{l}", NFC)
    add("bk", 2)
    add("kn", 1)
    add("qn", 1)
    add("bq", 8)
    add("invf", 1)
    add("sgn", 1)
    add("sink", 8)
    add("bo", 8)
    add("bv", 256)
    return lay, off


CST, NCST = _cst_layout()
CM = {"ident": (0, 128), "ones": (128, 128), "bd64": (256, 128), "perm": (384, 128),
      "mcur": (512, 512), "mprev": (1024, 512), "m3": (1536, 32)}
NCM = 1536 + 32
CF = {"identf": (0, 128), "sel127": (128, 128)}
NCF = 256


def _fm(v, nch):
    return np.ascontiguousarray(np.asarray(v, np.float32).reshape(nch, 128).T)


def _host_consts(inp):
    cst = np.zeros((128, NCST), np.float32)

    def put(name, arr):
        o, n = CST[name]
        cst[:, o:o + n] = np.asarray(arr, np.float32).reshape(128, n)

    put("a_norm", _fm(inp["a_norm"][0], 8))
    put("f_norm0", _fm(inp["f_norm"][0], 8))
    put("f_norm1", _fm(inp["f_norm"][1], 8))
    put("kv_norm", _fm(inp["kv_norm"], 8))
    put("b_norm", _fm(inp["b_norm"][0], 8))
    for k in range(4):
        put(f"cw{k}", _fm(inp["a_conv_w"][0][k], 32))
    put("cb", _fm(inp["a_conv_b"][0], 32))
    p = np.arange(128)
    v = np.zeros(128, np.float32)
    v[:96] = np.tile(inp["a_dt_bias"][0], 3)
    put("dtb3", v)
    v = np.zeros(128, np.float32)
    v[:96] = np.tile(inp["a_A_log"][0], 3)
    put("alog3", v)
    Dv = inp["a_D"][0]
    put("Dc", np.stack([Dv[2 * c + p // 64] for c in range(16)], axis=1))
    put("gn", _fm(inp["a_gnorm"][0], 16))
    for l in range(2):
        for k in range(3):
            put(f"fcw{l}_{k}", _fm(inp["f_conv_w"][l][k], NFC))
        put(f"fcb{l}", _fm(inp["f_conv_b"][l], NFC))
    put("bk", _fm(inp["b_kv"][:256], 2))
    put("kn", inp["k_norm"][p % 64])
    put("qn", inp["q_norm"][0][p % 64])
    bq = inp["b_q"][0]
    cols = []
    for pi in range(2):
        for i in range(4):
            g = 2 * pi + p // 64
            cols.append(bq[(g * 4 + i) * 64 + p % 64])
    put("bq", np.stack(cols, axis=1))
    half = 32
    inv_freq = (np.float32(10000.0) ** (-np.arange(half, dtype=np.float32) / np.float32(half))).astype(np.float32)
    put("invf", inv_freq[p % 32])
    put("sgn", np.where((p % 64) < 32, -1.0, 1.0))
    sk = inp["sinks"][0]
    cols = []
    for pi in range(2):
        for i in range(4):
            cols.append(sk[(2 * pi + p // 64) * 4 + i])
    put("sink", np.stack(cols, axis=1))
    put("bo", _fm(inp["b_o"][0], 8))
    put("bv", np.tile(inp["b_kv"][256:512][None, :], (128, 1)))

    cm = np.zeros((128, NCM), np.float32)
    cm[:, 0:128] = np.eye(128)
    cm[:, 128:256] = 1.0
    cm[:, 256:384] = (p[:, None] // 64 == p[None, :] // 64)
    partner = np.where((p % 64) < 32, p + 32, p - 32)
    pm = np.zeros((128, 128), np.float32)
    pm[partner, p] = 1.0
    cm[:, 384:512] = pm
    kk = p[:, None]
    qq = p[None, :]
    mcur = np.where(kk <= qq, 0.0, -30000.0)
    mprev = np.where(kk > qq, 0.0, -30000.0)
    cm[:, 512:1024] = np.tile(mcur, (1, 4))
    cm[:, 1024:1536] = np.tile(mprev, (1, 4))
    for h in range(32):
        for grp in range(3):
            cm[32 * grp + h, 1536 + h] = 1.0
    cmb = cm.astype(ml_dtypes.bfloat16)

    cf = np.zeros((128, NCF), np.float32)
    cf[:, 0:128] = np.eye(128)
    cf[127, 128:256] = 1.0
    return cst, cmb, cf


class Op:
    __slots__ = ("eng", "fn", "deps", "signal", "count", "semkey", "inc", "idx", "grp")


class Builder:
    ENGS = ["pe", "act", "dve", "pool", "sp"]

    def __init__(self):
        self.nc = bass.Bass("TRN2", target_bir_lowering=False)
        self.ops = {e: [] for e in self.ENGS}
        self.all_ops = []
        self.recs = {}
        self.dma_count = {}
        self.stack = ExitStack()
        self.rowbytes = {}

    def sb(self, name, shape, dt):
        name = "sb_" + name
        t = self.stack.enter_context(self.nc.sbuf_tensor(name, list(shape), dt))
        self.rowbytes[name] = int(np.prod(shape[1:])) * DT_SIZE[dt]
        return t

    def psum(self, name, shape, dt):
        name = "ps_" + name
        t = self.stack.enter_context(self.nc.psum_tensor(name, list(shape), dt))
        self.rowbytes[name] = int(np.prod(shape[1:])) * DT_SIZE[dt]
        return t

    def rect(self, ap):
        name = ap.tensor.name
        if name not in self.rowbytes:
            return None
        rb = self.rowbytes[name]
        esz = DT_SIZE[ap.dtype]
        offb = int(ap.offset) * esz
        p0 = offb // rb
        b0 = offb % rb
        dims = ap.ap
        npart = dims[0][1]
        ext = 0
        for st, n in dims[1:]:
            ext += (n - 1) * abs(st)
        b1 = b0 + (ext + 1) * esz
        ivs = None
        if name.startswith("ps_"):
            b0 = (b0 // 2048) * 2048
            b1 = ((b1 + 2047) // 2048) * 2048
        else:
            fd = [(abs(st), n) for st, n in dims[1:] if n > 1]
            if all(st > 0 for st, n in fd):
                if not fd:
                    ivs = ((b0, b0 + esz),)
                elif fd[-1][0] == 1:
                    cnt = 1
                    for st, n in fd[:-1]:
                        cnt *= n
                    if cnt <= 64:
                        starts = [0]
                        for st, n in fd[:-1]:
                            starts = [s0 + k * st for s0 in starts for k in range(n)]
                        run = fd[-1][1] * esz
                        raw = sorted((b0 + s0 * esz, b0 + s0 * esz + run) for s0 in starts)
                        merged = [list(raw[0])]
                        for lo_, hi_ in raw[1:]:
                            if lo_ <= merged[-1][1]:
                                merged[-1][1] = max(merged[-1][1], hi_)
                            else:
                                merged.append([lo_, hi_])
                        ivs = tuple((a, b_) for a, b_ in merged)
        return (name, p0, p0 + npart, b0, b1, ivs)

    @staticmethod
    def _ivs_overlap(a, b):
        i = j = 0
        while i < len(a) and j < len(b):
            if a[i][0] < b[j][1] and b[j][0] < a[i][1]:
                return True
            if a[i][1] <= b[j][1]:
                i += 1
            else:
                j += 1
        return False

    def _deps_for(self, op, reads, writes):
        deps = set()
        for is_w, aps in ((False, reads), (True, writes)):
            for ap in aps:
                r = self.rect(ap)
                if r is None:
                    continue
                name, p0, p1, b0, b1, ivs = r
                is_ps = name.startswith("ps_")
                lst = self.recs.setdefault(name, [])
                keep = []
                for rec in lst:
                    ov = not (rec[1] <= p0 or p1 <= rec[0] or rec[3] <= b0 or b1 <= rec[2])
                    if ov and ivs is not None and rec[6] is not None:
                        ov = self._ivs_overlap(ivs, rec[6])
                    if ov and (is_w or rec[4]) and rec[5] is not op:
                        deps.add(rec[5])
                    box_in = p0 <= rec[0] and rec[1] <= p1 and b0 <= rec[2] and rec[3] <= b1
                    if is_ps:
                        covered = is_w and box_in
                    else:
                        covered = is_w and box_in and ivs is not None and (len(ivs) == 1 or ivs == rec[6])
                    same_read = (not is_w) and (not rec[4]) and rec[5].eng == op.eng and rec[5].semkey == op.semkey \
                        and rec[0] == p0 and rec[1] == p1 and rec[2] == b0 and rec[3] == b1 and rec[6] == ivs
                    if not covered and not same_read:
                        keep.append(rec)
                keep.append([p0, p1, b0, b1, is_w, op, ivs])
                self.recs[name] = keep
        return deps

    def add(self, eng, fn, reads, writes, dma_sem=None):
        op = Op()
        op.eng = eng
        op.fn = fn
        op.signal = dma_sem is not None
        op.semkey = dma_sem if dma_sem is not None else eng
        op.inc = 16 if dma_sem is not None else 1
        op.count = None
        op.grp = None
        op.idx = len(self.all_ops)
        deps = self._deps_for(op, [a for a in reads if a is not None], [a for a in writes if a is not None])
        op.deps = []
        for d in deps:
            if d.semkey == eng and dma_sem is None and d.inc == 1:
                if eng == "pe" or not SAME_ENGINE_SYNC:
                    continue
            op.deps.append(d)
            d.signal = True
        self.ops[eng].append(op)
        self.all_ops.append(op)
        return op

    def mm(self, out, lhsT, rhs, start=True, stop=True):
        return self.add("pe", lambda e: e.matmul(out, lhsT, rhs, start=start, stop=stop),
                        [lhsT, rhs] + ([] if start else [out]), [out])

    def tr(self, out, in_, ident):
        return self.add("pe", lambda e: e.transpose(out, in_, ident), [in_, ident], [out])

    def act(self, out, in_, func, bias=None, scale=1.0):
        rd = [in_]
        kw = {}
        if bias is not None:
            kw["bias"] = bias
            if not isinstance(bias, (int, float)):
                rd.append(bias)
        if not isinstance(scale, (int, float)):
            rd.append(scale)
        kw["scale"] = scale
        return self.add("act", lambda e: e.activation(out, in_, func, **kw), rd, [out])

    def tt(self, out, in0, in1, op, eng="dve"):
        return self.add(eng, lambda e: e.tensor_tensor(out, in0, in1, op), [in0, in1], [out])

    def ts(self, out, in0, s1, s2, op0, op1=None, eng="dve"):
        rd = [in0] + [s for s in (s1, s2) if s is not None and not isinstance(s, (int, float))]
        if op1 is None:
            return self.add(eng, lambda e: e.tensor_scalar(out, in0, s1, None, op0), rd, [out])
        return self.add(eng, lambda e: e.tensor_scalar(out, in0, s1, s2, op0, op1), rd, [out])

    def stt(self, out, in0, scalar, in1, op0, op1):
        rd = [in0, in1] + ([] if isinstance(scalar, (int, float)) else [scalar])
        return self.add("dve", lambda e: e.scalar_tensor_tensor(out, in0, scalar, in1, op0, op1), rd, [out])

    def copy(self, out, in_, eng="dve"):
        return self.add(eng, lambda e: e.tensor_copy(out, in_), [in_], [out])

    def memset(self, ap, val, eng="dve"):
        return self.add(eng, lambda e: e.memset(ap, val), [], [ap])

    def dma(self, queue, out, in_, sem):
        return self.add(queue, lambda e: e.dma_start(out=out, in_=in_), [in_], [out], dma_sem=sem)

    def emit(self, final_waits):
        nc = self.nc
        cnt = {}
        for op in self.all_ops:
            if op.signal:
                cnt[op.semkey] = cnt.get(op.semkey, 0) + op.inc
                op.count = cnt[op.semkey]
        semkeys = sorted(cnt.keys())
        sems = {k: self.stack.enter_context(nc.semaphore("s_" + k)) for k in semkeys}
        self.nsems = len(sems)
        block = self.stack.enter_context(nc.Block())
        stats = {}

        def run(eng, e):
            waited = {}
            nw = 0
            for op in self.ops[eng]:
                need = {}
                for d in op.deps:
                    dc = (d.grp or d).count
                    if need.get(d.semkey, 0) < dc:
                        need[d.semkey] = dc
                for k, v in need.items():
                    if waited.get(k, 0) < v:
                        e.wait_ge(sems[k], v)
                        waited[k] = v
                        nw += 1
                ins = op.fn(e)
                if op.signal:
                    ins.then_inc(sems[op.semkey], op.inc)
            if eng == "sp":
                for k in final_waits:
                    if k in cnt:
                        e.wait_ge(sems[k], cnt[k])
            stats[eng] = (len(self.ops[eng]), nw)

        @block.tensor
        def _(e):
            run("pe", e)

        @block.scalar
        def _(e):
            run("act", e)

        @block.vector
        def _(e):
            run("dve", e)

        @block.gpsimd
        def _(e):
            run("pool", e)

        @block.sync
        def _(e):
            run("sp", e)

        self.stats = stats


def build(stage=99, dumps=()):
    K = Builder()
    nc = K.nc
    dram = {}

    def din(name, shape, dt=F32):
        dram[name] = nc.dram_tensor(name, list(shape), dt, kind="ExternalInput").ap()
        return dram[name]

    x_d = din("x", [T, D])
    pos_d = din("pos", [1, T], I32)
    cst_d = din("cst", [128, NCST])
    cm_d = din("cm", [128, NCM], BF16)
    cf_d = din("cf", [128, NCF])
    w_inproj = din("a_in_proj", [D, 6176])
    w_outproj = din("a_out_proj", [SSM_INNER, D])
    w_kv = din("w_kv", [D, 512])
    w_q = din("w_q", [D, D])
    w_o = din("w_o", [D, D])
    w_fin = [din(f"f_w_in{l}", [D, 2 * FFN]) for l in range(2)]
    w_fdn = [din(f"f_w_down{l}", [FFN, D]) for l in range(2)]
    out_d = nc.dram_tensor("out", [T, D], F32, kind="ExternalOutput").ap()
    dump_d = {}

    cst = K.sb("cst", [128, NCST], F32)
    cm = K.sb("cm", [128, NCM], BF16)
    cf = K.sb("cf", [128, NCF], F32)
    xT = K.sb("xT", [128, 8, TB], F32)
    kT = K.sb("kT", [128, 2, T], BF16)
    vtok = K.sb("vtok", [128, 16, 256], BF16)
    prevF = K.sb("prevF", [128, 2048], F32)
    prevB = K.sb("prevB", [128, 2048], BF16)
    halo_m = K.sb("halo_m", [128, 32, 4], F32)
    halo_f = K.sb("halo_f", [128, 2, NFC, 2], F32)
    wring = K.sb("wring", [128, NSLOT, SLOT_ELEMS], BF16)
    zsT = K.sb("zsT", [128, 16, TB], BF16)
    xcT = K.sb("xcT", [128, 16, TB], BF16)
    BT = K.sb("BT", [128, 8, TB], BF16)
    CT = K.sb("CT", [128, 8, TB], BF16)
    hT = K.sb("hT", [128, 8, TB], BF16)
    xdb = K.sb("xdb", [128, 2, 2, 2048], BF16)
    Ssb = K.sb("Ssb", [128, TB], F32)
    tokS = K.sb("tokS", [128, 4, 96], F32)
    nacum = K.sb("nacum", [128, 4, 32], F32)
    cdec = K.sb("cdec", [128, 2, 32], F32)
    CBb = K.sb("CBb", [128, 2, 384], BF16)
    rawb = K.sb("rawb", [128, 3, 516], F32)
    accb = K.sb("accb", [128, 2, TB], F32)
    sqb = K.sb("sqb", [128, 2, TB], BF16)
    lnb = K.sb("lnb", [128, TB], F32)
    rstd = K.sb("rstd", [128, TB], F32)
    Eb = K.sb("Eb", [128, 4, 384], BF16)
    Erow = K.sb("Erow", [128, 4, 256], BF16)
    Wt = K.sb("Wt", [128, 4, 384], BF16)
    Cs = K.sb("Cs", [128, 4, 256], BF16)
    ytmp = K.sb("ytmp", [128, 2, 256], F32)
    stmp = K.sb("stmp", [128, 256], F32)
    def f32v(t, dt=F32):
        return t[:].bitcast(dt).rearrange("p (a two) n -> p a (two n)", two=2)
    BTf, CTf, hTf = f32v(BT), f32v(CT), f32v(hT)
    angb = BTf[:, 0:2, :]
    cosT = BTf[:, 2, :]
    sinT = BTf[:, 3, :]
    kfb = CTf[:, 0, :]
    qraw = CTf[:, 2:4, :]
    t1b = hTf[:, 0:2, :]
    dtot = hTf[:, 2, :]
    qnb = K.sb("qnb", [128, 2, TB], BF16)
    xtk = K.sb("xtk", [128, 2, TB], BF16)
    sgb = xtk
    ibuf = K.sb("ibuf", [128, TB], I32)
    posi = ibuf[:]
    kib = ibuf[:]
    AH = K.sb("AH", [128, TB], BF16)
    PTb = rawb[:].bitcast(BF16).rearrange("p a n -> p (a n)")[:, 0:4 * TB].rearrange("p (a n) -> p a n", a=4)
    esink = K.sb("esink", [128, 8], F32)
    avec = K.sb("avec", [128, 1], F32)
    ps = K.psum("ps", [128, 8, 512], F32)

    Btok = hT
    xin = xdb
    hkvT = zsT
    qT = xcT
    attT = xcT

    def C(name, j=None, n=None):
        o, w = CST[name]
        if j is None:
            return cst[:, o:o + w]
        return cst[:, o + j:o + j + (n or 1)]

    def CMv(name):
        o, w = CM[name]
        return cm[:, o:o + w]

    ident = CMv("ident")
    ones = CMv("ones")
    identf = cf[:, 0:128]
    sel127 = cf[:, 128:256]

    bank_ctr = [0]

    def bank():
        b = bank_ctr[0] % 8
        bank_ctr[0] += 1
        return b

    def dump(name, ap):
        if name in dumps:
            dump_d[name] = nc.dram_tensor("dbg_" + name, list(ap.shape), ap.dtype, kind="ExternalOutput").ap()
            K.dma("sp", dump_d[name], ap, "dbg_" + name)

    wsched = []
    wissued = [0]
    wviews = {}

    def wplan(key, parts, shape):
        wsched.append((key, parts, shape))

    def wview(slot, shape):
        n = int(np.prod(shape[1:]))
        v = wring[:, slot, 0:n]
        if len(shape) == 3:
            v = v.rearrange("p (a b) -> p a b", a=shape[1])
        return v

    def wissue_upto(i):
        while wissued[0] <= min(i, len(wsched) - 1):
            j = wissued[0]
            key, parts, shape = wsched[j]
            slot = j % NSLOT
            v = wview(slot, shape)
            wviews[key] = v
            tile_ops = [K.dma("pool", fn(v), src, f"w{slot}") for fn, src in parts]
            for o_ in tile_ops:
                o_.grp = tile_ops[-1]
            wissued[0] += 1

    wpos = {}

    def wget(key):
        i = wpos[key]
        wissue_upto(i + NSLOT - 1)
        return wviews[key]

    def kc_view(w, c0, n, kcs=8, r0=0):
        return w[r0:r0 + kcs * 128, c0:c0 + n].rearrange("(kc p) n -> p kc n", p=128)

    for b in range(NBLK):
        wplan(("inp", b, 12), [((lambda v, r=r: v[:, :, r * 32:(r + 1) * 32]), kc_view(w_inproj, 6144, 32)) for r in range(3)],
              [128, 8, 96])
        for t in INP_ORDER:
            wplan(("inp", b, t), [(lambda v: v, kc_view(w_inproj, t * 512, 512))], [128, 8, 512])
        for t in range(4):
            wplan(("outp", b, t), [(lambda v: v, kc_view(w_outproj, t * 256, 256, kcs=16))], [128, 16, 256])
        for l in range(2):
            if l == 1:
                wplan(("wkv", b), [(lambda v: v, kc_view(w_kv, 0, 512))], [128, 8, 512])
                for pi in range(2):
                    parts = []
                    for i in range(4):
                        for half in range(2):
                            col0 = ((2 * pi + half) * 4 + i) * 64
                            parts.append(((lambda v, i=i, half=half: v[:, :, i * 128 + half * 64:i * 128 + half * 64 + 64]),
                                          kc_view(w_q, col0, 64)))
                    wplan(("wq", b, pi), parts, [128, 8, 512])
                for t in range(2):
                    parts = []
                    for half in range(2):
                        for pi in range(2):
                            r0 = ((2 * pi + half) * 4) * 64
                            src = w_o[r0:r0 + 256, t * 512:(t + 1) * 512].rearrange("(i d) n -> d i n", i=4)
                            parts.append(((lambda v, half=half, pi=pi: v[half * 64:(half + 1) * 64, pi * 4:(pi + 1) * 4, :]), src))
                    wplan(("wo", b, t), parts, [128, 8, 512])
            for t in range(11):
                parts = [((lambda v: v[:, :, 0:256]), kc_view(w_fin[l], t * 256, 256)),
                         ((lambda v: v[:, :, 256:512]), kc_view(w_fin[l], FFN + t * 256, 256))]
                wplan(("fin", b, l, t), parts, [128, 8, 512])
            for cg in range(2):
                for part, (k0, nk) in enumerate(FDN_PARTS):
                    wplan(("fdn", b, l, cg, part), [(lambda v: v, kc_view(w_fdn[l], cg * 512, 512, kcs=nk, r0=k0 * 128))],
                          [128, nk, 512])
    for i, (key, _, _) in enumerate(wsched):
        wpos[key] = i

    K.dma("sp", cst[:], cst_d, "c0")
    K.dma("sp", cm[:], cm_d, "c1")
    K.dma("sp", cf[:], cf_d, "c2")
    K.memset(halo_m[:], 0.0)
    K.memset(halo_f[:], 0.0)
    K.memset(prevF[:], 0.0)
    K.memset(prevB[:], 0.0)
    K.memset(Ssb[:], 0.0)
    K.memset(AH[:], 0.0)
    K.act(avec[0:96, :], C("alog3")[0:96, :], AF.Exp)
    K.ts(avec[0:96, :], avec[0:96, :], -1.0, None, OP.mult)
    K.act(esink[:], C("sink"), AF.Exp)

    def rms_stats(src_fn, nch, lhs, scale):
        bk = bank()
        for c in range(nch):
            if c % 2 == 0:
                K.act(sqb[:, 0, :], src_fn(c), AF.Square)
            else:
                K.tt(sqb[:, 1, :], src_fn(c), src_fn(c), OP.mult)
            K.mm(ps[:, bk, :], lhs, sqb[:, c % 2, :], start=(c == 0), stop=(c == nch - 1))
        K.act(lnb[:], ps[:, bk, :], AF.Ln, bias=epsb[:, 0:1], scale=scale)
        K.act(rstd[:], lnb[:], AF.Exp, scale=-0.5)

    epsb = K.sb("epsb", [128, 1], F32)
    K.memset(epsb[:], EPS)
    onesf = K.sb("onesf", [128, 256], F32)
    K.memset(onesf[:], 1.0)

    def conv_taps(dst_acc, src_ps, raw, ntap, wname_fn, bname):
        K.act(dst_acc, src_ps, AF.Identity, bias=bname, scale=wname_fn(ntap - 1))
        for k in range(ntap - 2, -1, -1):
            K.stt(dst_acc, raw[:, k:k + TB], wname_fn(k), dst_acc, OP.mult, OP.add)

    def ffn(l, b, mid_hook=None):
        rms_stats(lambda c: xT[:, c, :], 8, ones, 1.0 / D)
        for c in range(8):
            K.stt(hT[:, c, :], xT[:, c, :], C(f"f_norm{l}", c), rstd[:], OP.mult, OP.mult)
        uT = zsT
        def u(j):
            return zsT[:, j, :] if j < 16 else xcT[:, j - 16, :]
        pend = []
        for t in range(11):
            w = wget(("fin", b, l, t))
            for jj in range(2):
                j = 2 * t + jj
                bg_, bv_ = bank(), bank()
                for kc in range(8):
                    K.mm(ps[:, bg_, :], w[:, kc, jj * 128:(jj + 1) * 128], hT[:, kc, :], start=(kc == 0), stop=(kc == 7))
                for kc in range(8):
                    K.mm(ps[:, bv_, :], w[:, kc, 256 + jj * 128:256 + (jj + 1) * 128], hT[:, kc, :], start=(kc == 0), stop=(kc == 7))
                r = j % 3
                K.act(rawb[:, r, 0:2], halo_f[:, l, j, :], AF.Copy)
                K.act(rawb[:, r, 2:2 + TB], ps[:, bg_, :], AF.Copy)
                K.act(halo_f[:, l, j, :], ps[:, bg_, TB - 2:TB], AF.Copy)
                acc = accb[:, j % 2, :]
                conv_taps(acc, ps[:, bg_, :], rawb[:, r, :], 3, lambda k: C(f"fcw{l}_{k}", j), C(f"fcb{l}", j))
                for f in pend:
                    f()
                pend.clear()

                def fin(j=j, acc=acc, bv_=bv_):
                    K.act(sgb[:, j % 2, :], acc, AF.Silu)
                    K.tt(u(j), sgb[:, j % 2, :], ps[:, bv_, :], OP.mult)
                pend.append(fin)
        for f in pend:
            f()
        if mid_hook is not None:
            mid_hook()
        for cg in range(2):
            banks = [bank() for _ in range(4)]
            for part, (k0, nk) in enumerate(FDN_PARTS):
                w = wget(("fdn", b, l, cg, part))
                for jj in range(4):
                    for kk in range(nk):
                        kc = k0 + kk
                        K.mm(ps[:, banks[jj], :], w[:, kk, jj * 128:(jj + 1) * 128], u(kc), start=(kc == 0), stop=(kc == 21))
            for jj in range(4):
                oc = 4 * cg + jj
                K.tt(xT[:, oc, :], xT[:, oc, :], ps[:, banks[jj], :], OP.add)

    for b in range(NBLK):
        t0 = b * TB
        def rope_tables():
            K.dma("sp", posi, pos_d[0:1, t0:t0 + TB].partition_broadcast(128), "posd")
            posf = t1b[:, 0, :]
            K.copy(posf, posi)
            TWO_PI = 2.0 * math.pi
            C1 = 6.28125
            C2 = TWO_PI - C1
            for which, shift, dst in ((0, 0.0, sinT), (1, math.pi / 2.0, cosT)):
                ang = angb[:, which, :]
                K.ts(ang, posf, C("invf"), shift, OP.mult, OP.add)
                K.ts(kfb, ang, 1.0 / TWO_PI, None, OP.mult)
                K.copy(kib, kfb)
                K.copy(kfb, kib)
                K.stt(ang, kfb, -C1, ang, OP.mult, OP.add)
                K.stt(ang, kfb, -C2, ang, OP.mult, OP.add)
                K.ts(kfb, ang, math.pi, -TWO_PI, OP.is_gt, OP.mult)
                K.tt(ang, ang, kfb, OP.add)
                K.ts(kfb, ang, -math.pi, TWO_PI, OP.is_lt, OP.mult)
                K.tt(ang, ang, kfb, OP.add)
                K.ts(ang, ang, math.pi, -math.pi, OP.min, OP.max)
                K.act(dst[:], ang, AF.Sin)
            K.ts(sinT, sinT, C("sgn"), None, OP.mult)

        xin_v = xin[:].bitcast(F32).rearrange("p a b (c n) -> p (a b c) n", n=1024)
        if b == 0:
            for tl in range(4):
                K.dma("sp", xin_v[:, tl, :], x_d[t0 + tl * 128:t0 + (tl + 1) * 128, :], f"xin{tl}")
        for c in range(8):
            bk = bank()
            for tl in range(4):
                K.tr(ps[:, bk, tl * 128:(tl + 1) * 128], xin_v[:, tl, c * 128:(c + 1) * 128], identf)
            K.act(xT[:, c, :], ps[:, bk, :], AF.Copy)
        if b == 0:
            dump("xT0", xT[:])
        if stage <= 0:
            continue
        rms_stats(lambda c: xT[:, c, :], 8, ones, 1.0 / D)
        for c in range(8):
            K.stt(hT[:, c, :], xT[:, c, :], C("a_norm", c), rstd[:], OP.mult, OP.mult)
        if b == 0:
            dump("hT0", hT[:])
        w = wget(("inp", b, 12))
        bk = bank()
        for kc in range(8):
            K.mm(ps[0:96, bk, :], w[:, kc, :], hT[:, kc, :], start=(kc == 0), stop=(kc == 7))
        K.act(lnb[0:96, :], ps[0:96, bk, :], AF.Exp, bias=C("dtb3")[0:96, :])
        dt3 = accb[0:96, 0, :]
        K.act(dt3, lnb[0:96, :], AF.Ln, bias=1.0)
        a3 = accb[0:96, 1, :]
        K.ts(a3, dt3, avec[0:96, :], None, OP.mult)
        ac3 = lnb[0:96, :]
        for ck in range(2):
            K.add("dve", (lambda e, ck=ck: e.tensor_tensor_scan(ac3[:, ck * 256:(ck + 1) * 256], onesf[0:96, 0:256],
                                                              a3[:, ck * 256:(ck + 1) * 256], 0.0, OP.mult, OP.add)),
                  [onesf[0:96, 0:256], a3[:, ck * 256:(ck + 1) * 256]], [ac3[:, ck * 256:(ck + 1) * 256]])
        K.copy(Ssb[0:32, :], dt3[0:32, :])
        K.copy(Ssb[64:96, :], ac3[64:96, :])
        for ck in range(2):
            sl = slice(ck * 256, (ck + 1) * 256)
            K.act(Ssb[32:64, sl], ac3[32:64, sl], AF.Exp, bias=ac3[32:64, ck * 256 + 255:ck * 256 + 256], scale=-1.0)
        K.tt(Ssb[32:64, :], Ssb[32:64, :], dt3[32:64, :], OP.mult)
        for g0 in (0, 32, 64):
            K.copy(AH[g0:g0 + 32, :], ac3[g0:g0 + 32, :])
        for g0 in (32, 64):
            K.tt(accb[g0:g0 + 32, 0, :], ac3[g0:g0 + 32, :], AH[g0:g0 + 32, :], OP.subtract)
            K.copy(AH[g0:g0 + 32, :], accb[g0:g0 + 32, 0, :])
        K.tt(accb[64:96, 1, :], accb[64:96, 0, :], AH[64:96, :], OP.subtract)
        K.copy(AH[64:96, :], accb[64:96, 1, :])
        def tok_decay():
            for tl in range(4):
                bk = bank()
                K.tr(ps[:, bk, 0:96], Ssb[0:96, tl * 128:(tl + 1) * 128], identf[0:96, 0:96])
                K.copy(tokS[:, tl, :], ps[:, bk, 0:96])
                K.ts(nacum[:, tl, :], tokS[:, tl, 64:96], -1.0, None, OP.mult)

        def chunk_decay():
            for ck in range(2):
                bk = bank()
                K.mm(ps[:, bk, 0:32], sel127, tokS[:, 2 * ck + 1, 64:96])
                K.act(cdec[:, ck, :], ps[:, bk, 0:32], AF.Exp)

        def xd_tiles(tl):
            for c4 in range(4):
                bk = bank()
                pb = ps[:, bk, :].bitcast(BF16)
                for q in range(4):
                    K.tr(pb[:, q * 128:(q + 1) * 128], xcT[:, c4 * 4 + q, tl * 128:(tl + 1) * 128], ident)
                xk = xtk[:, c4 % 2, :]
                K.copy(xk, pb[:, 0:512])
                src = xk.rearrange("p (h d) -> p h d", d=64)
                for which, col0 in ((0, 0), (1, 32)):
                    dst = xdb[:, which, tl % 2, c4 * 512:(c4 + 1) * 512].rearrange("p (h d) -> p h d", d=64)
                    sc = tokS[:, tl, col0 + c4 * 8:col0 + c4 * 8 + 8].unsqueeze(2).to_broadcast([128, 8, 64])
                    K.tt(dst, src, sc, OP.mult)

        pend = []
        for t in INP_ORDER:
            w = wget(("inp", b, t))
            for jj in range(4):
                ch = 4 * t + jj
                bk = bank()
                for kc in range(8):
                    K.mm(ps[:, bk, :], w[:, kc, jj * 128:(jj + 1) * 128], hT[:, kc, :], start=(kc == 0), stop=(kc == 7))
                if ch < 16:
                    K.act(zsT[:, ch, :], ps[:, bk, :], AF.Silu)
                else:
                    j = ch - 16
                    r = j % 3
                    K.copy(rawb[:, r, 0:3], halo_m[:, j, 0:3], eng="pool")
                    K.act(rawb[:, r, 3:3 + TB], ps[:, bk, :], AF.Copy)
                    K.copy(halo_m[:, j, 0:3], rawb[:, r, TB:TB + 3], eng="pool")
                    acc = accb[:, j % 2, :]
                    conv_taps(acc, ps[:, bk, :], rawb[:, r, :], 4, lambda k: C(f"cw{k}", j), C("cb", j))
                    if j < 16:
                        dst = xcT[:, j, :]
                    elif j < 24:
                        dst = BT[:, j - 16, :]
                    else:
                        dst = CT[:, j - 24, :]
                    for f in pend:
                        f()
                    pend.clear()
                    pend.append(lambda dst=dst, acc=acc: K.act(dst, acc, AF.Silu))
            if t == INP_ORDER[1]:
                tok_decay()
            if t == 9:
                xd_tiles(0)
                xd_tiles(1)
        for f in pend:
            f()
        pend.clear()
        if b == 0:
            dump("S0", Ssb[:])
            dump("xcT0", xcT[:])
            dump("zsT0", zsT[:])
            dump("BT0", BT[:])
        if stage <= 1:
            continue
        chunk_decay()
        Btok_v = Btok[:].rearrange("p a b -> p (a b)").rearrange("p (t n) -> p t n", t=4)
        for tl in range(4):
            for g4 in range(2):
                bk = bank()
                pb = ps[:, bk, :].bitcast(BF16)
                for q in range(4):
                    K.tr(pb[:, q * 128:(q + 1) * 128], BT[:, g4 * 4 + q, tl * 128:(tl + 1) * 128], ident)
                K.copy(Btok_v[:, tl, g4 * 512:(g4 + 1) * 512], pb[:, 0:512])

        for ck in range(2):
            tl0, tl1 = 2 * ck, 2 * ck + 1
            c0 = ck * 256
            if ck > 0:
                xd_tiles(tl0)
                xd_tiles(tl1)
            items = [(g, hh) for g in range(8) for hh in range(4)]

            def s1(i):
                g, hh = items[i]
                h = 4 * g + hh
                gcb = g if i == 0 else (g + 1 if (hh == 3 and i + 1 < len(items)) else None)
                if gcb is not None:
                    bcb = gcb % 2
                    K.mm(ps[:, bcb, 0:256], BT[:, gcb, c0:c0 + 128], CT[:, gcb, c0:c0 + 256])
                    K.mm(ps[:, bcb, 256:384], BT[:, gcb, c0 + 128:c0 + 256], CT[:, gcb, c0 + 128:c0 + 256])
                oh = cm[:, 1536 + h:1537 + h].to_broadcast([128, 128])
                pa4 = ps[:, 5 + i % 3, :].rearrange("p (t r n) -> p t r n", t=2, r=2)
                rhs4 = AH[:, c0:c0 + 256].rearrange("p (t n) -> p t n", t=2).unsqueeze(2).to_broadcast([128, 2, 2, 128])
                K.mm(pa4, oh, rhs4, start=True, stop=False)
                K.mm(ps[:, 5 + i % 3, 128:256], ident, CMv("mcur")[:, 0:128], start=False, stop=False)
                K.mm(ps[:, 5 + i % 3, 384:512], ident, CMv("mcur")[:, 0:128], start=False, stop=True)

            def s2(i):
                g, hh = items[i]
                h = 4 * g + hh
                r = i % 4
                ba = 5 + i % 3
                bcb = g % 2
                if i == 0:
                    K.act(CBb[:, g % 2, :], ps[:, bcb, 0:384], AF.Copy)
                pa4 = ps[:, ba, :].rearrange("p (t r n) -> p t r n", t=2, r=2)
                K.act(Erow[:, r, :].rearrange("p (t n) -> p t n", t=2), pa4[:, :, 0, :], AF.Exp)
                K.act(Eb[:, r, 0:256], ps[:, ba, 128:384], AF.Exp, bias=nacum[:, tl0, h:h + 1])
                K.act(Eb[:, r, 256:384], ps[:, ba, 384:512], AF.Exp, bias=nacum[:, tl1, h:h + 1])
                if hh == 3 and i + 1 < len(items):
                    K.act(CBb[:, (g + 1) % 2, :], ps[:, (g + 1) % 2, 0:384], AF.Copy)
                K.tt(Wt[:, r, :], CBb[:, g % 2, :], Eb[:, r, :], OP.mult)
                K.tt(Cs[:, r, :], CT[:, g, c0:c0 + 256], Erow[:, r, :], OP.mult)

            def s3(i):
                g, hh = items[i]
                h = 4 * g + hh
                r = i % 4
                hp = h % 2
                by = 2 + ((h // 2) % 2)
                po = ps[hp * 64:(hp + 1) * 64, by, 0:256]
                K.mm(po, xdb[:, 0, 0, h * 64:(h + 1) * 64], Wt[:, r, 0:256], start=True, stop=False)
                K.mm(ps[hp * 64:(hp + 1) * 64, by, 128:256], xdb[:, 0, 1, h * 64:(h + 1) * 64], Wt[:, r, 256:384], start=False, stop=False)
                K.mm(po, prevB[:, h * 64:(h + 1) * 64], Cs[:, r, :], start=False, stop=True)
                if hh % 2 == 1:
                    c = h // 2
                    yt = ytmp[:, c % 2, :]
                    K.stt(yt, xcT[:, c, c0:c0 + 256], C("Dc", c), ps[:, by, 0:256], OP.mult, OP.add)
                    K.tt(zsT[:, c, c0:c0 + 256], yt, zsT[:, c, c0:c0 + 256], OP.mult)
                if hh == 3:
                    bs = 4
                    K.mm(ps[:, bs, 0:256], Btok_v[:, tl0, g * 128:(g + 1) * 128], xdb[:, 1, 0, g * 256:(g + 1) * 256], start=True, stop=False)
                    K.mm(ps[:, bs, 0:256], Btok_v[:, tl1, g * 128:(g + 1) * 128], xdb[:, 1, 1, g * 256:(g + 1) * 256], start=False, stop=True)
                    pv = prevF[:, g * 256:(g + 1) * 256]
                    K.tt(stmp[:].rearrange("p (h d) -> p h d", d=64), pv.rearrange("p (h d) -> p h d", d=64),
                         cdec[:, ck, 4 * g:4 * g + 4].unsqueeze(2).to_broadcast([128, 4, 64]), OP.mult, eng="pool")
                    K.tt(pv, stmp[:], ps[:, bs, 0:256], OP.add)
                    K.copy(prevB[:, g * 256:(g + 1) * 256], pv)

            NI = len(items)
            for i in range(NI + 2):
                if i < NI:
                    s1(i)
                if 0 <= i - 1 < NI:
                    s2(i - 1)
                if 0 <= i - 2 < NI:
                    s3(i - 2)
        if b == 0:
            dump("yg0", zsT[:])
        if b + 1 < NBLK and stage > 4:
            for tl in range(4):
                K.dma("sp", xin_v[:, tl, :], x_d[t0 + TB + tl * 128:t0 + TB + (tl + 1) * 128, :], f"xin{tl}")
        gl = [lnb[:], accb[:, 0, :]]
        gr = [rstd[:], accb[:, 1, :]]
        gpend = []
        for g in range(8):
            bkg = bank()
            for c in range(2):
                K.act(sqb[:, c, :], zsT[:, 2 * g + c, :], AF.Square)
                K.mm(ps[:, bkg, :], ones, sqb[:, c, :], start=(c == 0), stop=(c == 1))
            for f in gpend:
                f()
            gpend.clear()

            def gapply(g=g, bkg=bkg):
                K.act(gl[g % 2], ps[:, bkg, :], AF.Ln, bias=epsb[:, 0:1], scale=1.0 / 256)
                K.act(gr[g % 2], gl[g % 2], AF.Exp, scale=-0.5)
                for c in (2 * g, 2 * g + 1):
                    K.stt(zsT[:, c, :], zsT[:, c, :], C("gn", c), gr[g % 2], OP.mult, OP.mult)
            gpend.append(gapply)
        for f in gpend:
            f()
        gpend.clear()
        for t in range(4):
            w = wget(("outp", b, t))
            for jj in range(2):
                bk = bank()
                for kc in range(16):
                    K.mm(ps[:, bk, :], w[:, kc, jj * 128:(jj + 1) * 128], zsT[:, kc, :], start=(kc == 0), stop=(kc == 15))
                oc = 2 * t + jj
                K.tt(xT[:, oc, :], xT[:, oc, :], ps[:, bk, :], OP.add)
        if b == 0:
            dump("x1T0", xT[:])
        if stage <= 2:
            continue
        ffn(0, b, mid_hook=(rope_tables if stage > 3 else None))
        if b == 0:
            dump("x2T0", xT[:])
        if stage <= 3:
            continue
        rms_stats(lambda c: xT[:, c, :], 8, ones, 1.0 / D)
        for c in range(8):
            K.stt(hkvT[:, c, :], xT[:, c, :], C("kv_norm", c), rstd[:], OP.mult, OP.mult)
        for c in range(8):
            K.stt(hkvT[:, 8 + c, :], xT[:, c, :], C("b_norm", c), rstd[:], OP.mult, OP.mult)
        lnbs = [lnb[:], accb[:, 0, :], ytmp[:].rearrange("p a n -> p (a n)")]
        rstds = [rstd[:], accb[:, 1, :], Ssb[:]]
        qraws = [qraw[:, 0, :], qraw[:, 1, :], kfb]
        sqs = [sqb[:, 0, :], sqb[:, 1, :], xtk[:, 0, :]]
        qnbs = [qnb[:, 0, :], qnb[:, 1, :], xtk[:, 1, :]]
        t1s = [t1b[:, 0, :], t1b[:, 1, :], hTf[:, 3, :]]
        pa, pb = [], []

        def hnr_step(new_pair):
            for f in pb:
                f()
            pb.clear()
            for fa, fb in pa:
                fa()
                pb.append(fb)
            pa.clear()
            if new_pair is not None:
                pa.append(new_pair)

        def hnr_p1(mm_fn, bias_col, gain_col, dst, r):
            bk = bank()
            mm_fn(bk)
            K.act(qraws[r], ps[:, bk, :], AF.Identity, bias=bias_col)
            K.act(sqs[r], qraws[r], AF.Square)

            def p2a():
                b2 = bank()
                K.mm(ps[:, b2, :], CMv("bd64"), sqs[r])
                K.act(lnbs[r], ps[:, b2, :], AF.Ln, bias=epsb[:, 0:1], scale=1.0 / 64)
                K.act(rstds[r], lnbs[r], AF.Exp, scale=-0.5)
                K.stt(qnbs[r], qraws[r], gain_col, rstds[r], OP.mult, OP.mult)

            def p2b():
                b3 = bank()
                K.mm(ps[:, b3, :], CMv("perm"), qnbs[r])
                K.tt(t1s[r], qnbs[r], cosT, OP.mult)
                K.tt(qraws[r], ps[:, b3, :], sinT, OP.mult)
                K.tt(dst, t1s[r], qraws[r], OP.add)
            hnr_step((p2a, p2b))

        w = wget(("wkv", b))
        wkv_t = w
        cidx = 0
        for oc in range(2):
            def mmk(bk, oc=oc):
                for kc in range(8):
                    K.mm(ps[:, bk, :], wkv_t[:, kc, oc * 128:(oc + 1) * 128], hkvT[:, kc, :], start=(kc == 0), stop=(kc == 7))
            hnr_p1(mmk, C("bk", oc), C("kn"), kT[:, oc, t0:t0 + TB], cidx % 3)
            cidx += 1
        for tl in range(4):
            bk = bank()
            for kc in range(8):
                K.mm(ps[:, bk, 0:256], hkvT[:, kc, tl * 128:(tl + 1) * 128], w[:, kc, 256:512], start=(kc == 0), stop=(kc == 7))
            K.tt(vtok[:, 4 * b + tl, :], ps[:, bk, 0:256], C("bv"), OP.add)
        qT_v = qT[:, 0:8, :].rearrange("p (pi i) t -> p pi i t", pi=2)
        attT_v = attT[:, 8:16, :].rearrange("p (pi i) t -> p pi i t", pi=2)
        apend = []
        npend = []
        cnts = [0, 0, cidx]

        def qproj(pi, wq_t, i_list):
            for i in i_list:
                def mmq(bk, i=i, wq_t=wq_t):
                    for kc in range(8):
                        K.mm(ps[:, bk, :], wq_t[:, kc, i * 128:(i + 1) * 128], hkvT[:, 8 + kc, :], start=(kc == 0), stop=(kc == 7))
                hnr_p1(mmq, C("bq", pi * 4 + i), C("qn"), qT_v[:, pi, i, :], cnts[2] % 3)
                cnts[2] += 1

        def core(pi):
            for nl in range(4):
                n = 4 * b + nl
                qs = slice(nl * 128, (nl + 1) * 128)
                bo_, bd_ = 4 + cnts[1] % 2, 6 + cnts[1] % 2
                cnts[1] += 1
                units = []
                for half in range(2):
                    tiles = ([(n - 1, CMv("mprev"))] if n > 0 else []) + [(n, CMv("mcur"))]
                    for ti, (kt, msk) in enumerate(tiles):
                        units.append((half, ti, kt, msk, len(tiles)))
                for ui, (half, ti, kt, msk, ntl) in enumerate(units):
                    g = 2 * pi + half
                    P0 = half * 64
                    ucnt = cnts[0]
                    bs_ = ucnt % 3
                    K.mm(ps[:, bs_, :].rearrange("p (i q) -> p i q", i=4), kT[P0:P0 + 64, pi, kt * 128:(kt + 1) * 128],
                         qT_v[P0:P0 + 64, pi, :, qs], start=True, stop=False)
                    K.mm(ps[:, bs_, :], ident, msk, start=False, stop=True)
                    while len(apend) > 1:
                        apend.pop(0)()

                    def fin(bs_=bs_, ucnt=ucnt, P0=P0, g=g, kt=kt, ti=ti, ntl=ntl, bo_=bo_, bd_=bd_, last_unit=(ui == len(units) - 1), pi=pi, qs=qs, last_block=(nl == 3)):
                        pt = PTb[:, ucnt % 4, :]
                        K.act(pt, ps[:, bs_, :], AF.Exp, scale=0.125)
                        first, last = (ti == 0), (ti == ntl - 1)
                        K.mm(ps[P0:P0 + 64, bo_, :], vtok[:, kt, g * 64:(g + 1) * 64], pt, start=first, stop=last)
                        K.mm(ps[P0:P0 + 64, bd_, :], ones[:, 0:64], pt, start=first, stop=last)
                        for fn_ in npend:
                            fn_()
                        npend.clear()
                        if last_unit:
                            def norm():
                                for i in range(4):
                                    K.ts(dtot[:, i * 128:(i + 1) * 128], ps[:, bd_, i * 128:(i + 1) * 128], esink[:, pi * 4 + i:pi * 4 + i + 1], None, OP.add)
                                if last_block:
                                    K.act(lnb[:], dtot, AF.Ln)
                                    K.act(dtot, lnb[:], AF.Exp, scale=-1.0)
                                else:
                                    K.add("dve", (lambda e: e.reciprocal(dtot, dtot)), [dtot], [dtot])
                                for i in range(4):
                                    K.tt(attT_v[:, pi, i, qs], ps[:, bo_, i * 128:(i + 1) * 128], dtot[:, i * 128:(i + 1) * 128], OP.mult)
                            npend.append(norm)
                    apend.append(fin)
                    cnts[0] += 1
            for f in apend:
                f()
            apend.clear()
            for fn_ in npend:
                fn_()
            npend.clear()

        wq0 = wget(("wq", b, 0))
        qproj(0, wq0, [0, 1, 2, 3])
        wq1 = wget(("wq", b, 1))
        qproj(1, wq1, [0, 1])
        core(0)
        qproj(1, wq1, [2, 3])
        hnr_step(None)
        hnr_step(None)
        core(1)
        if b == 0:
            dump("kT0", kT[:, :, 0:TB])
            dump("qT0", qT[:, 0:8, :])
            dump("v0", vtok[:, 0:4, :])
        if b == 0:
            dump("att0", attT[:, 8:16, :])
        for t in range(2):
            w = wget(("wo", b, t))
            banks = [bank() for _ in range(4)]
            for kc in range(8):
                for jj in range(4):
                    K.mm(ps[:, banks[jj], :], w[:, kc, jj * 128:(jj + 1) * 128], attT[:, 8 + kc, :], start=(kc == 0), stop=(kc == 7))
            for jj in range(4):
                oc = 4 * t + jj
                K.stt(xT[:, oc, :], ps[:, banks[jj], :], C("bo", oc), xT[:, oc, :], OP.add, OP.add)
        if b == 0:
            dump("x3T0", xT[:])
        if stage <= 4:
            continue
        ffn(1, b)
        xo_v = zsT[:].bitcast(F32).rearrange("p (a b) n -> p a (b n)", b=4)
        for tl in range(4):
            for c2 in range(2):
                bk = bank()
                for q in range(4):
                    c = 4 * c2 + q
                    K.tr(ps[:, bk, q * 128:(q + 1) * 128], xT[:, c, tl * 128:(tl + 1) * 128], identf)
                K.act(xo_v[:, tl, c2 * 512:(c2 + 1) * 512], ps[:, bk, :], AF.Copy)
            K.dma("sp", out_d[t0 + tl * 128:t0 + (tl + 1) * 128, :], xo_v[:, tl, :], f"out{tl}")

    finals = [f"out{tl}" for tl in range(4)] + ["dbg_" + n for n in dump_d]
    K.emit(finals)
    return K, dump_d


_CACHE = {}


def _prep_inputs(inputs):
    inp = {k: np.asarray(v) for k, v in inputs.items()}
    cst, cmb, cf = _host_consts(inp)
    shared = {
        "cst": cst, "cm": cmb, "cf": cf,
        "a_in_proj": np.ascontiguousarray(inp["a_in_proj"][0]),
        "a_out_proj": np.ascontiguousarray(inp["a_out_proj"][0]),
        "w_kv": np.ascontiguousarray(inp["w_kv"]),
        "w_q": np.ascontiguousarray(inp["w_q"][0]),
        "w_o": np.ascontiguousarray(inp["w_o"][0]),
        "f_w_in0": np.ascontiguousarray(inp["f_w_in"][0]),
        "f_w_in1": np.ascontiguousarray(inp["f_w_in"][1]),
        "f_w_down0": np.ascontiguousarray(inp["f_w_down"][0]),
        "f_w_down1": np.ascontiguousarray(inp["f_w_down"][1]),
    }
    maps = []
    for c in range(8):
        m = dict(shared)
        m["x"] = np.ascontiguousarray(inp["x"][c])
        m["pos"] = np.ascontiguousarray(inp["positions"][c].reshape(1, T).astype(np.int32))
        maps.append(m)
    return maps


def kernel(**inputs):
    maps = _prep_inputs(inputs)
    K, _ = build()
    res = run_bass_kernel_spmd(K.nc, maps, core_ids=list(range(8)))
    return np.stack([np.asarray(r["out"], dtype=np.float32) for r in res.results], axis=0)
```

```python
import math
from contextlib import ExitStack

import numpy as np
import ml_dtypes
import concourse.bass as bass
import concourse.mybir as mybir
from concourse.bass_utils import run_bass_kernel_spmd

F32 = mybir.dt.float32
BF16 = mybir.dt.bfloat16
I32 = mybir.dt.int32
AF = mybir.ActivationFunctionType
OP = mybir.AluOpType
DT_SIZE = {F32: 4, BF16: 2, I32: 4}

T = 2048
D = 1024
TB = 512
NBLK = T // TB
EPS = 1e-5
SSM_INNER = 2048
NHEAD = 32
FFN = 2816
NFC = FFN // 128
SAME_ENGINE_SYNC = True
NSLOT = 5
INP_ORDER = [4, 5, 0, 6, 7, 1, 8, 9, 2, 10, 11, 3]
FDN_PARTS = [(0, 6), (6, 6), (12, 5), (17, 5)]
SLOT_ELEMS = 8 * 512


def _cst_layout():
    lay = {}
    off = 0

    def add(name, n):
        nonlocal off
        lay[name] = (off, n)
        off += n

    for nm in ["a_norm", "f_norm0", "f_norm1", "kv_norm", "b_norm"]:
        add(nm, 8)
    for k in range(4):
        add(f"cw{k}", 32)
    add("cb", 32)
    add("dtb3", 1)
    add("alog3", 1)
    add("Dc", 16)
    add("gn", 16)
    for l in range(2):
        for k in range(3):
            add(f"fcw{l}_{k}", NFC)
        add(f"fcb{l}", NFC)
    add("bk", 2)
    add("kn", 1)
    add("qn", 1)
    add("bq", 8)
    add("invf", 1)
    add("sgn", 1)
    add("sink", 8)
    add("bo", 8)
    add("bv", 256)
    return lay, off


CST, NCST = _cst_layout()
CM = {"ident": (0, 128), "ones": (128, 128), "bd64": (256, 128), "perm": (384, 128),
      "mcur": (512, 512), "mprev": (1024, 512), "m3": (1536, 32)}
NCM = 1536 + 32
CF = {"identf": (0, 128), "sel127": (128, 128)}
NCF = 256


def _fm(v, nch):
    return np.ascontiguousarray(np.asarray(v, np.float32).reshape(nch, 128).T)


def _host_consts(inp):
    cst = np.zeros((128, NCST), np.float32)

    def put(name, arr):
        o, n = CST[name]
        cst[:, o:o + n] = np.asarray(arr, np.float32).reshape(128, n)

    put("a_norm", _fm(inp["a_norm"][0], 8))
    put("f_norm0", _fm(inp["f_norm"][0], 8))
    put("f_norm1", _fm(inp["f_norm"][1], 8))
    put("kv_norm", _fm(inp["kv_norm"], 8))
    put("b_norm", _fm(inp["b_norm"][0], 8))
    for k in range(4):
        put(f"cw{k}", _fm(inp["a_conv_w"][0][k], 32))
    put("cb", _fm(inp["a_conv_b"][0], 32))
    p = np.arange(128)
    v = np.zeros(128, np.float32)
    v[:96] = np.tile(inp["a_dt_bias"][0], 3)
    put("dtb3", v)
    v = np.zeros(128, np.float32)
    v[:96] = np.tile(inp["a_A_log"][0], 3)
    put("alog3", v)
    Dv = inp["a_D"][0]
    put("Dc", np.stack([Dv[2 * c + p // 64] for c in range(16)], axis=1))
    put("gn", _fm(inp["a_gnorm"][0], 16))
    for l in range(2):
        for k in range(3):
            put(f"fcw{l}_{k}", _fm(inp["f_conv_w"][l][k], NFC))
        put(f"fcb{l}", _fm(inp["f_conv_b"][l], NFC))
    put("bk", _fm(inp["b_kv"][:256], 2))
    put("kn", inp["k_norm"][p % 64])
    put("qn", inp["q_norm"][0][p % 64])
    bq = inp["b_q"][0]
    cols = []
    for pi in range(2):
        for i in range(4):
            g = 2 * pi + p // 64
            cols.append(bq[(g * 4 + i) * 64 + p % 64])
    put("bq", np.stack(cols, axis=1))
    half = 32
    inv_freq = (np.float32(10000.0) ** (-np.arange(half, dtype=np.float32) / np.float32(half))).astype(np.float32)
    put("invf", inv_freq[p % 32])
    put("sgn", np.where((p % 64) < 32, -1.0, 1.0))
    sk = inp["sinks"][0]
    cols = []
    for pi in range(2):
        for i in range(4):
            cols.append(sk[(2 * pi + p // 64) * 4 + i])
    put("sink", np.stack(cols, axis=1))
    put("bo", _fm(inp["b_o"][0], 8))
    put("bv", np.tile(inp["b_kv"][256:512][None, :], (128, 1)))

    cm = np.zeros((128, NCM), np.float32)
    cm[:, 0:128] = np.eye(128)
    cm[:, 128:256] = 1.0
    cm[:, 256:384] = (p[:, None] // 64 == p[None, :] // 64)
    partner = np.where((p % 64) < 32, p + 32, p - 32)
    pm = np.zeros((128, 128), np.float32)
    pm[partner, p] = 1.0
    cm[:, 384:512] = pm
    kk = p[:, None]
    qq = p[None, :]
    mcur = np.where(kk <= qq, 0.0, -30000.0)
    mprev = np.where(kk > qq, 0.0, -30000.0)
    cm[:, 512:1024] = np.tile(mcur, (1, 4))
    cm[:, 1024:1536] = np.tile(mprev, (1, 4))
    for h in range(32):
        for grp in range(3):
            cm[32 * grp + h, 1536 + h] = 1.0
    cmb = cm.astype(ml_dtypes.bfloat16)

    cf = np.zeros((128, NCF), np.float32)
    cf[:, 0:128] = np.eye(128)
    cf[127, 128:256] = 1.0
    return cst, cmb, cf


class Op:
    __slots__ = ("eng", "fn", "deps", "signal", "count", "semkey", "inc", "idx", "grp")


class Builder:
    ENGS = ["pe", "act", "dve", "pool", "sp"]

    def __init__(self):
        self.nc = bass.Bass("TRN2", target_bir_lowering=False)
        self.ops = {e: [] for e in self.ENGS}
        self.all_ops = []
        self.recs = {}
        self.dma_count = {}
        self.stack = ExitStack()
        self.rowbytes = {}

    def sb(self, name, shape, dt):
        name = "sb_" + name
        t = self.stack.enter_context(self.nc.sbuf_tensor(name, list(shape), dt))
        self.rowbytes[name] = int(np.prod(shape[1:])) * DT_SIZE[dt]
        return t

    def psum(self, name, shape, dt):
        name = "ps_" + name
        t = self.stack.enter_context(self.nc.psum_tensor(name, list(shape), dt))
        self.rowbytes[name] = int(np.prod(shape[1:])) * DT_SIZE[dt]
        return t

    def rect(self, ap):
        name = ap.tensor.name
        if name not in self.rowbytes:
            return None
        rb = self.rowbytes[name]
        esz = DT_SIZE[ap.dtype]
        offb = int(ap.offset) * esz
        p0 = offb // rb
        b0 = offb % rb
        dims = ap.ap
        npart = dims[0][1]
        ext = 0
        for st, n in dims[1:]:
            ext += (n - 1) * abs(st)
        b1 = b0 + (ext + 1) * esz
        ivs = None
        if name.startswith("ps_"):
            b0 = (b0 // 2048) * 2048
            b1 = ((b1 + 2047) // 2048) * 2048
        else:
            fd = [(abs(st), n) for st, n in dims[1:] if n > 1]
            if all(st > 0 for st, n in fd):
                if not fd:
                    ivs = ((b0, b0 + esz),)
                elif fd[-1][0] == 1:
                    cnt = 1
                    for st, n in fd[:-1]:
                        cnt *= n
                    if cnt <= 64:
                        starts = [0]
                        for st, n in fd[:-1]:
                            starts = [s0 + k * st for s0 in starts for k in range(n)]
                        run = fd[-1][1] * esz
                        raw = sorted((b0 + s0 * esz, b0 + s0 * esz + run) for s0 in starts)
                        merged = [list(raw[0])]
                        for lo_, hi_ in raw[1:]:
                            if lo_ <= merged[-1][1]:
                                merged[-1][1] = max(merged[-1][1], hi_)
                            else:
                                merged.append([lo_, hi_])
                        ivs = tuple((a, b_) for a, b_ in merged)
        return (name, p0, p0 + npart, b0, b1, ivs)

    @staticmethod
    def _ivs_overlap(a, b):
        i = j = 0
        while i < len(a) and j < len(b):
            if a[i][0] < b[j][1] and b[j][0] < a[i][1]:
                return True
            if a[i][1] <= b[j][1]:
                i += 1
            else:
                j += 1
        return False

    def _deps_for(self, op, reads, writes):
        deps = set()
        for is_w, aps in ((False, reads), (True, writes)):
            for ap in aps:
                r = self.rect(ap)
                if r is None:
                    continue
                name, p0, p1, b0, b1, ivs = r
                is_ps = name.startswith("ps_")
                lst = self.recs.setdefault(name, [])
                keep = []
                for rec in lst:
                    ov = not (rec[1] <= p0 or p1 <= rec[0] or rec[3] <= b0 or b1 <= rec[2])
                    if ov and ivs is not None and rec[6] is not None:
                        ov = self._ivs_overlap(ivs, rec[6])
                    if ov and (is_w or rec[4]) and rec[5] is not op:
                        deps.add(rec[5])
                    box_in = p0 <= rec[0] and rec[1] <= p1 and b0 <= rec[2] and rec[3] <= b1
                    if is_ps:
                        covered = is_w and box_in
                    else:
                        covered = is_w and box_in and ivs is not None and (len(ivs) == 1 or ivs == rec[6])
                    same_read = (not is_w) and (not rec[4]) and rec[5].eng == op.eng and rec[5].semkey == op.semkey \
                        and rec[0] == p0 and rec[1] == p1 and rec[2] == b0 and rec[3] == b1 and rec[6] == ivs
                    if not covered and not same_read:
                        keep.append(rec)
                keep.append([p0, p1, b0, b1, is_w, op, ivs])
                self.recs[name] = keep
        return deps

    def add(self, eng, fn, reads, writes, dma_sem=None):
        op = Op()
        op.eng = eng
        op.fn = fn
        op.signal = dma_sem is not None
        op.semkey = dma_sem if dma_sem is not None else eng
        op.inc = 16 if dma_sem is not None else 1
        op.count = None
        op.grp = None
        op.idx = len(self.all_ops)
        deps = self._deps_for(op, [a for a in reads if a is not None], [a for a in writes if a is not None])
        op.deps = []
        for d in deps:
            if d.semkey == eng and dma_sem is None and d.inc == 1:
                if eng == "pe" or not SAME_ENGINE_SYNC:
                    continue
            op.deps.append(d)
            d.signal = True
        self.ops[eng].append(op)
        self.all_ops.append(op)
        return op

    def mm(self, out, lhsT, rhs, start=True, stop=True):
        return self.add("pe", lambda e: e.matmul(out, lhsT, rhs, start=start, stop=stop),
                        [lhsT, rhs] + ([] if start else [out]), [out])

    def tr(self, out, in_, ident):
        return self.add("pe", lambda e: e.transpose(out, in_, ident), [in_, ident], [out])

    def act(self, out, in_, func, bias=None, scale=1.0):
        rd = [in_]
        kw = {}
        if bias is not None:
            kw["bias"] = bias
            if not isinstance(bias, (int, float)):
                rd.append(bias)
        if not isinstance(scale, (int, float)):
            rd.append(scale)
        kw["scale"] = scale
        return self.add("act", lambda e: e.activation(out, in_, func, **kw), rd, [out])

    def tt(self, out, in0, in1, op, eng="dve"):
        return self.add(eng, lambda e: e.tensor_tensor(out, in0, in1, op), [in0, in1], [out])

    def ts(self, out, in0, s1, s2, op0, op1=None, eng="dve"):
        rd = [in0] + [s for s in (s1, s2) if s is not None and not isinstance(s, (int, float))]
        if op1 is None:
            return self.add(eng, lambda e: e.tensor_scalar(out, in0, s1, None, op0), rd, [out])
        return self.add(eng, lambda e: e.tensor_scalar(out, in0, s1, s2, op0, op1), rd, [out])

    def stt(self, out, in0, scalar, in1, op0, op1):
        rd = [in0, in1] + ([] if isinstance(scalar, (int, float)) else [scalar])
        return self.add("dve", lambda e: e.scalar_tensor_tensor(out, in0, scalar, in1, op0, op1), rd, [out])

    def copy(self, out, in_, eng="dve"):
        return self.add(eng, lambda e: e.tensor_copy(out, in_), [in_], [out])

    def memset(self, ap, val, eng="dve"):
        return self.add(eng, lambda e: e.memset(ap, val), [], [ap])

    def dma(self, queue, out, in_, sem):
        return self.add(queue, lambda e: e.dma_start(out=out, in_=in_), [in_], [out], dma_sem=sem)

    def emit(self, final_waits):
        nc = self.nc
        cnt = {}
        for op in self.all_ops:
            if op.signal:
                cnt[op.semkey] = cnt.get(op.semkey, 0) + op.inc
                op.count = cnt[op.semkey]
        semkeys = sorted(cnt.keys())
        sems = {k: self.stack.enter_context(nc.semaphore("s_" + k)) for k in semkeys}
        self.nsems = len(sems)
        block = self.stack.enter_context(nc.Block())
        stats = {}

        def run(eng, e):
            waited = {}
            nw = 0
            for op in self.ops[eng]:
                need = {}
                for d in op.deps:
                    dc = (d.grp or d).count
                    if need.get(d.semkey, 0) < dc:
                        need[d.semkey] = dc
                for k, v in need.items():
                    if waited.get(k, 0) < v:
                        e.wait_ge(sems[k], v)
                        waited[k] = v
                        nw += 1
                ins = op.fn(e)
                if op.signal:
                    ins.then_inc(sems[op.semkey], op.inc)
            if eng == "sp":
                for k in final_waits:
                    if k in cnt:
                        e.wait_ge(sems[k], cnt[k])
            stats[eng] = (len(self.ops[eng]), nw)

        @block.tensor
        def _(e):
            run("pe", e)

        @block.scalar
        def _(e):
            run("act", e)

        @block.vector
        def _(e):
            run("dve", e)

        @block.gpsimd
        def _(e):
            run("pool", e)

        @block.sync
        def _(e):
            run("sp", e)

        self.stats = stats


def build(stage=99, dumps=()):
    K = Builder()
    nc = K.nc
    dram = {}

    def din(name, shape, dt=F32):
        dram[name] = nc.dram_tensor(name, list(shape), dt, kind="ExternalInput").ap()
        return dram[name]

    x_d = din("x", [T, D])
    pos_d = din("pos", [1, T], I32)
    cst_d = din("cst", [128, NCST])
    cm_d = din("cm", [128, NCM], BF16)
    cf_d = din("cf", [128, NCF])
    w_inproj = din("a_in_proj", [D, 6176])
    w_outproj = din("a_out_proj", [SSM_INNER, D])
    w_kv = din("w_kv", [D, 512])
    w_q = din("w_q", [D, D])
    w_o = din("w_o", [D, D])
    w_fin = [din(f"f_w_in{l}", [D, 2 * FFN]) for l in range(2)]
    w_fdn = [din(f"f_w_down{l}", [FFN, D]) for l in range(2)]
    out_d = nc.dram_tensor("out", [T, D], F32, kind="ExternalOutput").ap()
    dump_d = {}

    cst = K.sb("cst", [128, NCST], F32)
    cm = K.sb("cm", [128, NCM], BF16)
    cf = K.sb("cf", [128, NCF], F32)
    xT = K.sb("xT", [128, 8, TB], F32)
    kT = K.sb("kT", [128, 2, T], BF16)
    vtok = K.sb("vtok", [128, 16, 256], BF16)
    prevF = K.sb("prevF", [128, 2048], F32)
    prevB = K.sb("prevB", [128, 2048], BF16)
    halo_m = K.sb("halo_m", [128, 32, 4], F32)
    halo_f = K.sb("halo_f", [128, 2, NFC, 2], F32)
    wring = K.sb("wring", [128, NSLOT, SLOT_ELEMS], BF16)
    zsT = K.sb("zsT", [128, 16, TB], BF16)
    xcT = K.sb("xcT", [128, 16, TB], BF16)
    BT = K.sb("BT", [128, 8, TB], BF16)
    CT = K.sb("CT", [128, 8, TB], BF16)
    hT = K.sb("hT", [128, 8, TB], BF16)
    xdb = K.sb("xdb", [128, 2, 2, 2048], BF16)
    Ssb = K.sb("Ssb", [128, TB], F32)
    tokS = K.sb("tokS", [128, 4, 96], F32)
    nacum = K.sb("nacum", [128, 4, 32], F32)
    cdec = K.sb("cdec", [128, 2, 32], F32)
    CBb = K.sb("CBb", [128, 2, 384], BF16)
    rawb = K.sb("rawb", [128, 3, 516], F32)
    accb = K.sb("accb", [128, 2, TB], F32)
    sqb = K.sb("sqb", [128, 2, TB], BF16)
    lnb = K.sb("lnb", [128, TB], F32)
    rstd = K.sb("rstd", [128, TB], F32)
    Eb = K.sb("Eb", [128, 4, 384], BF16)
    Erow = K.sb("Erow", [128, 4, 256], BF16)
    Wt = K.sb("Wt", [128, 4, 384], BF16)
    Cs = K.sb("Cs", [128, 4, 256], BF16)
    ytmp = K.sb("ytmp", [128, 2, 256], F32)
    stmp = K.sb("stmp", [128, 256], F32)
    def f32v(t, dt=F32):
        return t[:].bitcast(dt).rearrange("p (a two) n -> p a (two n)", two=2)
    BTf, CTf, hTf = f32v(BT), f32v(CT), f32v(hT)
    angb = BTf[:, 0:2, :]
    cosT = BTf[:, 2, :]
    sinT = BTf[:, 3, :]
    kfb = CTf[:, 0, :]
    qraw = CTf[:, 2:4, :]
    t1b = hTf[:, 0:2, :]
    dtot = hTf[:, 2, :]
    qnb = K.sb("qnb", [128, 2, TB], BF16)
    xtk = K.sb("xtk", [128, 2, TB], BF16)
    sgb = xtk
    ibuf = K.sb("ibuf", [128, TB], I32)
    posi = ibuf[:]
    kib = ibuf[:]
    AH = K.sb("AH", [128, TB], BF16)
    PTb = rawb[:].bitcast(BF16).rearrange("p a n -> p (a n)")[:, 0:4 * TB].rearrange("p (a n) -> p a n", a=4)
    esink = K.sb("esink", [128, 8], F32)
    avec = K.sb("avec", [128, 1], F32)
    ps = K.psum("ps", [128, 8, 512], F32)

    Btok = hT
    xin = xdb
    hkvT = zsT
    qT = xcT
    attT = xcT

    def C(name, j=None, n=None):
        o, w = CST[name]
        if j is None:
            return cst[:, o:o + w]
        return cst[:, o + j:o + j + (n or 1)]

    def CMv(name):
        o, w = CM[name]
        return cm[:, o:o + w]

    ident = CMv("ident")
    ones = CMv("ones")
    identf = cf[:, 0:128]
    sel127 = cf[:, 128:256]

    bank_ctr = [0]

    def bank():
        b = bank_ctr[0] % 8
        bank_ctr[0] += 1
        return b

    def dump(name, ap):
        if name in dumps:
            dump_d[name] = nc.dram_tensor("dbg_" + name, list(ap.shape), ap.dtype, kind="ExternalOutput").ap()
            K.dma("sp", dump_d[name], ap, "dbg_" + name)

    wsched = []
    wissued = [0]
    wviews = {}

    def wplan(key, parts, shape):
        wsched.append((key, parts, shape))

    def wview(slot, shape):
        n = int(np.prod(shape[1:]))
        v = wring[:, slot, 0:n]
        if len(shape) == 3:
            v = v.rearrange("p (a b) -> p a b", a=shape[1])
        return v

    def wissue_upto(i):
        while wissued[0] <= min(i, len(wsched) - 1):
            j = wissued[0]
            key, parts, shape = wsched[j]
            slot = j % NSLOT
            v = wview(slot, shape)
            wviews[key] = v
            tile_ops = [K.dma("pool", fn(v), src, f"w{slot}") for fn, src in parts]
            for o_ in tile_ops:
                o_.grp = tile_ops[-1]
            wissued[0] += 1

    wpos = {}

    def wget(key):
        i = wpos[key]
        wissue_upto(i + NSLOT - 1)
        return wviews[key]

    def kc_view(w, c0, n, kcs=8, r0=0):
        return w[r0:r0 + kcs * 128, c0:c0 + n].rearrange("(kc p) n -> p kc n", p=128)

    for b in range(NBLK):
        wplan(("inp", b, 12), [((lambda v, r=r: v[:, :, r * 32:(r + 1) * 32]), kc_view(w_inproj, 6144, 32)) for r in range(3)],
              [128, 8, 96])
        for t in INP_ORDER:
            wplan(("inp", b, t), [(lambda v: v, kc_view(w_inproj, t * 512, 512))], [128, 8, 512])
        for t in range(4):
            wplan(("outp", b, t), [(lambda v: v, kc_view(w_outproj, t * 256, 256, kcs=16))], [128, 16, 256])
        for l in range(2):
            if l == 1:
                wplan(("wkv", b), [(lambda v: v, kc_view(w_kv, 0, 512))], [128, 8, 512])
                for pi in range(2):
                    parts = []
                    for i in range(4):
                        for half in range(2):
                            col0 = ((2 * pi + half) * 4 + i) * 64
                            parts.append(((lambda v, i=i, half=half: v[:, :, i * 128 + half * 64:i * 128 + half * 64 + 64]),
                                          kc_view(w_q, col0, 64)))
                    wplan(("wq", b, pi), parts, [128, 8, 512])
                for t in range(2):
                    parts = []
                    for half in range(2):
                        for pi in range(2):
                            r0 = ((2 * pi + half) * 4) * 64
                            src = w_o[r0:r0 + 256, t * 512:(t + 1) * 512].rearrange("(i d) n -> d i n", i=4)
                            parts.append(((lambda v, half=half, pi=pi: v[half * 64:(half + 1) * 64, pi * 4:(pi + 1) * 4, :]), src))
                    wplan(("wo", b, t), parts, [128, 8, 512])
            for t in range(11):
                parts = [((lambda v: v[:, :, 0:256]), kc_view(w_fin[l], t * 256, 256)),
                         ((lambda v: v[:, :, 256:512]), kc_view(w_fin[l], FFN + t * 256, 256))]
                wplan(("fin", b, l, t), parts, [128, 8, 512])
            for cg in range(2):
                for part, (k0, nk) in enumerate(FDN_PARTS):
                    wplan(("fdn", b, l, cg, part), [(lambda v: v, kc_view(w_fdn[l], cg * 512, 512, kcs=nk, r0=k0 * 128))],
                          [128, nk, 512])
    for i, (key, _, _) in enumerate(wsched):
        wpos[key] = i

    K.dma("sp", cst[:], cst_d, "c0")
    K.dma("sp", cm[:], cm_d, "c1")
    K.dma("sp", cf[:], cf_d, "c2")
    K.memset(halo_m[:], 0.0)
    K.memset(halo_f[:], 0.0)
    K.memset(prevF[:], 0.0)
    K.memset(prevB[:], 0.0)
    K.memset(Ssb[:], 0.0)
    K.memset(AH[:], 0.0)
    K.act(avec[0:96, :], C("alog3")[0:96, :], AF.Exp)
    K.ts(avec[0:96, :], avec[0:96, :], -1.0, None, OP.mult)
    K.act(esink[:], C("sink"), AF.Exp)

    def rms_stats(src_fn, nch, lhs, scale):
        bk = bank()
        for c in range(nch):
            if c % 2 == 0:
                K.act(sqb[:, 0, :], src_fn(c), AF.Square)
            else:
                K.tt(sqb[:, 1, :], src_fn(c), src_fn(c), OP.mult)
            K.mm(ps[:, bk, :], lhs, sqb[:, c % 2, :], start=(c == 0), stop=(c == nch - 1))
        K.act(lnb[:], ps[:, bk, :], AF.Ln, bias=epsb[:, 0:1], scale=scale)
        K.act(rstd[:], lnb[:], AF.Exp, scale=-0.5)

    epsb = K.sb("epsb", [128, 1], F32)
    K.memset(epsb[:], EPS)
    onesf = K.sb("onesf", [128, 256], F32)
    K.memset(onesf[:], 1.0)

    def conv_taps(dst_acc, src_ps, raw, ntap, wname_fn, bname):
        K.act(dst_acc, src_ps, AF.Identity, bias=bname, scale=wname_fn(ntap - 1))
        for k in range(ntap - 2, -1, -1):
            K.stt(dst_acc, raw[:, k:k + TB], wname_fn(k), dst_acc, OP.mult, OP.add)

    def ffn(l, b, mid_hook=None):
        rms_stats(lambda c: xT[:, c, :], 8, ones, 1.0 / D)
        for c in range(8):
            K.stt(hT[:, c, :], xT[:, c, :], C(f"f_norm{l}", c), rstd[:], OP.mult, OP.mult)
        uT = zsT
        def u(j):
            return zsT[:, j, :] if j < 16 else xcT[:, j - 16, :]
        pend = []
        for t in range(11):
            w = wget(("fin", b, l, t))
            for jj in range(2):
                j = 2 * t + jj
                bg_, bv_ = bank(), bank()
                for kc in range(8):
                    K.mm(ps[:, bg_, :], w[:, kc, jj * 128:(jj + 1) * 128], hT[:, kc, :], start=(kc == 0), stop=(kc == 7))
                for kc in range(8):
                    K.mm(ps[:, bv_, :], w[:, kc, 256 + jj * 128:256 + (jj + 1) * 128], hT[:, kc, :], start=(kc == 0), stop=(kc == 7))
                r = j % 3
                K.act(rawb[:, r, 0:2], halo_f[:, l, j, :], AF.Copy)
                K.act(rawb[:, r, 2:2 + TB], ps[:, bg_, :], AF.Copy)
                K.act(halo_f[:, l, j, :], ps[:, bg_, TB - 2:TB], AF.Copy)
                acc = accb[:, j % 2, :]
                conv_taps(acc, ps[:, bg_, :], rawb[:, r, :], 3, lambda k: C(f"fcw{l}_{k}", j), C(f"fcb{l}", j))
                for f in pend:
                    f()
                pend.clear()

                def fin(j=j, acc=acc, bv_=bv_):
                    K.act(sgb[:, j % 2, :], acc, AF.Silu)
                    K.tt(u(j), sgb[:, j % 2, :], ps[:, bv_, :], OP.mult)
                pend.append(fin)
        for f in pend:
            f()
        if mid_hook is not None:
            mid_hook()
        for cg in range(2):
            banks = [bank() for _ in range(4)]
            for part, (k0, nk) in enumerate(FDN_PARTS):
                w = wget(("fdn", b, l, cg, part))
                for jj in range(4):
                    for kk in range(nk):
                        kc = k0 + kk
                        K.mm(ps[:, banks[jj], :], w[:, kk, jj * 128:(jj + 1) * 128], u(kc), start=(kc == 0), stop=(kc == 21))
            for jj in range(4):
                oc = 4 * cg + jj
                K.tt(xT[:, oc, :], xT[:, oc, :], ps[:, banks[jj], :], OP.add)

    for b in range(NBLK):
        t0 = b * TB
        def rope_tables():
            K.dma("sp", posi, pos_d[0:1, t0:t0 + TB].partition_broadcast(128), "posd")
            posf = t1b[:, 0, :]
            K.copy(posf, posi)
            TWO_PI = 2.0 * math.pi
            C1 = 6.28125
            C2 = TWO_PI - C1
            for which, shift, dst in ((0, 0.0, sinT), (1, math.pi / 2.0, cosT)):
                ang = angb[:, which, :]
                K.ts(ang, posf, C("invf"), shift, OP.mult, OP.add)
                K.ts(kfb, ang, 1.0 / TWO_PI, None, OP.mult)
                K.copy(kib, kfb)
                K.copy(kfb, kib)
                K.stt(ang, kfb, -C1, ang, OP.mult, OP.add)
                K.stt(ang, kfb, -C2, ang, OP.mult, OP.add)
                K.ts(kfb, ang, math.pi, -TWO_PI, OP.is_gt, OP.mult)
                K.tt(ang, ang, kfb, OP.add)
                K.ts(kfb, ang, -math.pi, TWO_PI, OP.is_lt, OP.mult)
                K.tt(ang, ang, kfb, OP.add)
                K.ts(ang, ang, math.pi, -math.pi, OP.min, OP.max)
                K.act(dst[:], ang, AF.Sin)
            K.ts(sinT, sinT, C("sgn"), None, OP.mult)

        xin_v = xin[:].bitcast(F32).rearrange("p a b (c n) -> p (a b c) n", n=1024)
        if b == 0:
            for tl in range(4):
                K.dma("sp", xin_v[:, tl, :], x_d[t0 + tl * 128:t0 + (tl + 1) * 128, :], f"xin{tl}")
        for c in range(8):
            bk = bank()
            for tl in range(4):
                K.tr(ps[:, bk, tl * 128:(tl + 1) * 128], xin_v[:, tl, c * 128:(c + 1) * 128], identf)
            K.act(xT[:, c, :], ps[:, bk, :], AF.Copy)
        if b == 0:
            dump("xT0", xT[:])
        if stage <= 0:
            continue
        rms_stats(lambda c: xT[:, c, :], 8, ones, 1.0 / D)
        for c in range(8):
            K.stt(hT[:, c, :], xT[:, c, :], C("a_norm", c), rstd[:], OP.mult, OP.mult)
        if b == 0:
            dump("hT0", hT[:])
        w = wget(("inp", b, 12))
        bk = bank()
        for kc in range(8):
            K.mm(ps[0:96, bk, :], w[:, kc, :], hT[:, kc, :], start=(kc == 0), stop=(kc == 7))
        K.act(lnb[0:96, :], ps[0:96, bk, :], AF.Exp, bias=C("dtb3")[0:96, :])
        dt3 = accb[0:96, 0, :]
        K.act(dt3, lnb[0:96, :], AF.Ln, bias=1.0)
        a3 = accb[0:96, 1, :]
        K.ts(a3, dt3, avec[0:96, :], None, OP.mult)
        ac3 = lnb[0:96, :]
        for ck in range(2):
            K.add("dve", (lambda e, ck=ck: e.tensor_tensor_scan(ac3[:, ck * 256:(ck + 1) * 256], onesf[0:96, 0:256],
                                                              a3[:, ck * 256:(ck + 1) * 256], 0.0, OP.mult, OP.add)),
                  [onesf[0:96, 0:256], a3[:, ck * 256:(ck + 1) * 256]], [ac3[:, ck * 256:(ck + 1) * 256]])
        K.copy(Ssb[0:32, :], dt3[0:32, :])
        K.copy(Ssb[64:96, :], ac3[64:96, :])
        for ck in range(2):
            sl = slice(ck * 256, (ck + 1) * 256)
            K.act(Ssb[32:64, sl], ac3[32:64, sl], AF.Exp, bias=ac3[32:64, ck * 256 + 255:ck * 256 + 256], scale=-1.0)
        K.tt(Ssb[32:64, :], Ssb[32:64, :], dt3[32:64, :], OP.mult)
        for g0 in (0, 32, 64):
            K.copy(AH[g0:g0 + 32, :], ac3[g0:g0 + 32, :])
        for g0 in (32, 64):
            K.tt(accb[g0:g0 + 32, 0, :], ac3[g0:g0 + 32, :], AH[g0:g0 + 32, :], OP.subtract)
            K.copy(AH[g0:g0 + 32, :], accb[g0:g0 + 32, 0, :])
        K.tt(accb[64:96, 1, :], accb[64:96, 0, :], AH[64:96, :], OP.subtract)
        K.copy(AH[64:96, :], accb[64:96, 1, :])
        def tok_decay():
            for tl in range(4):
                bk = bank()
                K.tr(ps[:, bk, 0:96], Ssb[0:96, tl * 128:(tl + 1) * 128], identf[0:96, 0:96])
                K.copy(tokS[:, tl, :], ps[:, bk, 0:96])
                K.ts(nacum[:, tl, :], tokS[:, tl, 64:96], -1.0, None, OP.mult)

        def chunk_decay():
            for ck in range(2):
                bk = bank()
                K.mm(ps[:, bk, 0:32], sel127, tokS[:, 2 * ck + 1, 64:96])
                K.act(cdec[:, ck, :], ps[:, bk, 0:32], AF.Exp)

        def xd_tiles(tl):
            for c4 in range(4):
                bk = bank()
                pb = ps[:, bk, :].bitcast(BF16)
                for q in range(4):
                    K.tr(pb[:, q * 128:(q + 1) * 128], xcT[:, c4 * 4 + q, tl * 128:(tl + 1) * 128], ident)
                xk = xtk[:, c4 % 2, :]
                K.copy(xk, pb[:, 0:512])
                src = xk.rearrange("p (h d) -> p h d", d=64)
                for which, col0 in ((0, 0), (1, 32)):
                    dst = xdb[:, which, tl % 2, c4 * 512:(c4 + 1) * 512].rearrange("p (h d) -> p h d", d=64)
                    sc = tokS[:, tl, col0 + c4 * 8:col0 + c4 * 8 + 8].unsqueeze(2).to_broadcast([128, 8, 64])
                    K.tt(dst, src, sc, OP.mult)

        pend = []
        for t in INP_ORDER:
            w = wget(("inp", b, t))
            for jj in range(4):
                ch = 4 * t + jj
                bk = bank()
                for kc in range(8):
                    K.mm(ps[:, bk, :], w[:, kc, jj * 128:(jj + 1) * 128], hT[:, kc, :], start=(kc == 0), stop=(kc == 7))
                if ch < 16:
                    K.act(zsT[:, ch, :], ps[:, bk, :], AF.Silu)
                else:
                    j = ch - 16
                    r = j % 3
                    K.copy(rawb[:, r, 0:3], halo_m[:, j, 0:3], eng="pool")
                    K.act(rawb[:, r, 3:3 + TB], ps[:, bk, :], AF.Copy)
                    K.copy(halo_m[:, j, 0:3], rawb[:, r, TB:TB + 3], eng="pool")
                    acc = accb[:, j % 2, :]
                    conv_taps(acc, ps[:, bk, :], rawb[:, r, :], 4, lambda k: C(f"cw{k}", j), C("cb", j))
                    if j < 16:
                        dst = xcT[:, j, :]
                    elif j < 24:
                        dst = BT[:, j - 16, :]
                    else:
                        dst = CT[:, j - 24, :]
                    for f in pend:
                        f()
                    pend.clear()
                    pend.append(lambda dst=dst, acc=acc: K.act(dst, acc, AF.Silu))
            if t == INP_ORDER[1]:
                tok_decay()
            if t == 9:
                xd_tiles(0)
                xd_tiles(1)
        for f in pend:
            f()
        pend.clear()
        if b == 0:
            dump("S0", Ssb[:])
            dump("xcT0", xcT[:])
            dump("zsT0", zsT[:])
            dump("BT0", BT[:])
        if stage <= 1:
            continue
        chunk_decay()
        Btok_v = Btok[:].rearrange("p a b -> p (a b)").rearrange("p (t n) -> p t n", t=4)
        for tl in range(4):
            for g4 in range(2):
                bk = bank()
                pb = ps[:, bk, :].bitcast(BF16)
                for q in range(4):
                    K.tr(pb[:, q * 128:(q + 1) * 128], BT[:, g4 * 4 + q, tl * 128:(tl + 1) * 128], ident)
                K.copy(Btok_v[:, tl, g4 * 512:(g4 + 1) * 512], pb[:, 0:512])

        for ck in range(2):
            tl0, tl1 = 2 * ck, 2 * ck + 1
            c0 = ck * 256
            if ck > 0:
                xd_tiles(tl0)
                xd_tiles(tl1)
            items = [(g, hh) for g in range(8) for hh in range(4)]

            def s1(i):
                g, hh = items[i]
                h = 4 * g + hh
                gcb = g if i == 0 else (g + 1 if (hh == 3 and i + 1 < len(items)) else None)
                if gcb is not None:
                    bcb = gcb % 2
                    K.mm(ps[:, bcb, 0:256], BT[:, gcb, c0:c0 + 128], CT[:, gcb, c0:c0 + 256])
                    K.mm(ps[:, bcb, 256:384], BT[:, gcb, c0 + 128:c0 + 256], CT[:, gcb, c0 + 128:c0 + 256])
                oh = cm[:, 1536 + h:1537 + h].to_broadcast([128, 128])
                pa4 = ps[:, 5 + i % 3, :].rearrange("p (t r n) -> p t r n", t=2, r=2)
                rhs4 = AH[:, c0:c0 + 256].rearrange("p (t n) -> p t n", t=2).unsqueeze(2).to_broadcast([128, 2, 2, 128])
                K.mm(pa4, oh, rhs4, start=True, stop=False)
                K.mm(ps[:, 5 + i % 3, 128:256], ident, CMv("mcur")[:, 0:128], start=False, stop=False)
                K.mm(ps[:, 5 + i % 3, 384:512], ident, CMv("mcur")[:, 0:128], start=False, stop=True)

            def s2(i):
                g, hh = items[i]
                h = 4 * g + hh
                r = i % 4
                ba = 5 + i % 3
                bcb = g % 2
                if i == 0:
                    K.act(CBb[:, g % 2, :], ps[:, bcb, 0:384], AF.Copy)
                pa4 = ps[:, ba, :].rearrange("p (t r n) -> p t r n", t=2, r=2)
                K.act(Erow[:, r, :].rearrange("p (t n) -> p t n", t=2), pa4[:, :, 0, :], AF.Exp)
                K.act(Eb[:, r, 0:256], ps[:, ba, 128:384], AF.Exp, bias=nacum[:, tl0, h:h + 1])
                K.act(Eb[:, r, 256:384], ps[:, ba, 384:512], AF.Exp, bias=nacum[:, tl1, h:h + 1])
                if hh == 3 and i + 1 < len(items):
                    K.act(CBb[:, (g + 1) % 2, :], ps[:, (g + 1) % 2, 0:384], AF.Copy)
                K.tt(Wt[:, r, :], CBb[:, g % 2, :], Eb[:, r, :], OP.mult)
                K.tt(Cs[:, r, :], CT[:, g, c0:c0 + 256], Erow[:, r, :], OP.mult)

            def s3(i):
                g, hh = items[i]
                h = 4 * g + hh
                r = i % 4
                hp = h % 2
                by = 2 + ((h // 2) % 2)
                po = ps[hp * 64:(hp + 1) * 64, by, 0:256]
                K.mm(po, xdb[:, 0, 0, h * 64:(h + 1) * 64], Wt[:, r, 0:256], start=True, stop=False)
                K.mm(ps[hp * 64:(hp + 1) * 64, by, 128:256], xdb[:, 0, 1, h * 64:(h + 1) * 64], Wt[:, r, 256:384], start=False, stop=False)
                K.mm(po, prevB[:, h * 64:(h + 1) * 64], Cs[:, r, :], start=False, stop=True)
                if hh % 2 == 1:
                    c = h // 2
                    yt = ytmp[:, c % 2, :]
                    K.stt(yt, xcT[:, c, c0:c0 + 256], C("Dc", c), ps[:, by, 0:256], OP.mult, OP.add)
                    K.tt(zsT[:, c, c0:c0 + 256], yt, zsT[:, c, c0:c0 + 256], OP.mult)
                if hh == 3:
                    bs = 4
                    K.mm(ps[:, bs, 0:256], Btok_v[:, tl0, g * 128:(g + 1) * 128], xdb[:, 1, 0, g * 256:(g + 1) * 256], start=True, stop=False)
                    K.mm(ps[:, bs, 0:256], Btok_v[:, tl1, g * 128:(g + 1) * 128], xdb[:, 1, 1, g * 256:(g + 1) * 256], start=False, stop=True)
                    pv = prevF[:, g * 256:(g + 1) * 256]
                    K.tt(stmp[:].rearrange("p (h d) -> p h d", d=64), pv.rearrange("p (h d) -> p h d", d=64),
                         cdec[:, ck, 4 * g:4 * g + 4].unsqueeze(2).to_broadcast([128, 4, 64]), OP.mult, eng="pool")
                    K.tt(pv, stmp[:], ps[:, bs, 0:256], OP.add)
                    K.copy(prevB[:, g * 256:(g + 1) * 256], pv)

            NI = len(items)
            for i in range(NI + 3):
                if i < NI:
                    s1(i)
                if 0 <= i - 1 < NI:
                    s2(i - 1)
                if 0 <= i - 3 < NI:
                    s3(i - 3)
        if b == 0:
            dump("yg0", zsT[:])
        if b + 1 < NBLK and stage > 4:
            for tl in range(4):
                K.dma("sp", xin_v[:, tl, :], x_d[t0 + TB + tl * 128:t0 + TB + (tl + 1) * 128, :], f"xin{tl}")
        gl = [lnb[:], accb[:, 0, :]]
        gr = [rstd[:], accb[:, 1, :]]
        gpend = []
        for g in range(8):
            bkg = bank()
            for c in range(2):
                K.act(sqb[:, c, :], zsT[:, 2 * g + c, :], AF.Square)
                K.mm(ps[:, bkg, :], ones, sqb[:, c, :], start=(c == 0), stop=(c == 1))
            for f in gpend:
                f()
            gpend.clear()

            def gapply(g=g, bkg=bkg):
                K.act(gl[g % 2], ps[:, bkg, :], AF.Ln, bias=epsb[:, 0:1], scale=1.0 / 256)
                K.act(gr[g % 2], gl[g % 2], AF.Exp, scale=-0.5)
                for c in (2 * g, 2 * g + 1):
                    K.stt(zsT[:, c, :], zsT[:, c, :], C("gn", c), gr[g % 2], OP.mult, OP.mult)
            gpend.append(gapply)
        for f in gpend:
            f()
        gpend.clear()
        for t in range(4):
            w = wget(("outp", b, t))
            for jj in range(2):
                bk = bank()
                for kc in range(16):
                    K.mm(ps[:, bk, :], w[:, kc, jj * 128:(jj + 1) * 128], zsT[:, kc, :], start=(kc == 0), stop=(kc == 15))
                oc = 2 * t + jj
                K.tt(xT[:, oc, :], xT[:, oc, :], ps[:, bk, :], OP.add)
        if b == 0:
            dump("x1T0", xT[:])
        if stage <= 2:
            continue
        ffn(0, b, mid_hook=(rope_tables if stage > 3 else None))
        if b == 0:
            dump("x2T0", xT[:])
        if stage <= 3:
            continue
        rms_stats(lambda c: xT[:, c, :], 8, ones, 1.0 / D)
        for c in range(8):
            K.stt(hkvT[:, c, :], xT[:, c, :], C("kv_norm", c), rstd[:], OP.mult, OP.mult)
        for c in range(8):
            K.stt(hkvT[:, 8 + c, :], xT[:, c, :], C("b_norm", c), rstd[:], OP.mult, OP.mult)
        lnbs = [lnb[:], accb[:, 0, :], ytmp[:].rearrange("p a n -> p (a n)")]
        rstds = [rstd[:], accb[:, 1, :], Ssb[:]]
        qraws = [qraw[:, 0, :], qraw[:, 1, :], kfb]
        sqs = [sqb[:, 0, :], sqb[:, 1, :], xtk[:, 0, :]]
        qnbs = [qnb[:, 0, :], qnb[:, 1, :], xtk[:, 1, :]]
        t1s = [t1b[:, 0, :], t1b[:, 1, :], hTf[:, 3, :]]
        pa, pb = [], []

        def hnr_step(new_pair):
            for f in pb:
                f()
            pb.clear()
            for fa, fb in pa:
                fa()
                pb.append(fb)
            pa.clear()
            if new_pair is not None:
                pa.append(new_pair)

        def hnr_p1(mm_fn, bias_col, gain_col, dst, r):
            bk = bank()
            mm_fn(bk)
            K.act(qraws[r], ps[:, bk, :], AF.Identity, bias=bias_col)
            K.act(sqs[r], qraws[r], AF.Square)

            def p2a():
                b2 = bank()
                K.mm(ps[:, b2, :], CMv("bd64"), sqs[r])
                K.act(lnbs[r], ps[:, b2, :], AF.Ln, bias=epsb[:, 0:1], scale=1.0 / 64)
                K.act(rstds[r], lnbs[r], AF.Exp, scale=-0.5)
                K.stt(qnbs[r], qraws[r], gain_col, rstds[r], OP.mult, OP.mult)

            def p2b():
                b3 = bank()
                K.mm(ps[:, b3, :], CMv("perm"), qnbs[r])
                K.tt(t1s[r], qnbs[r], cosT, OP.mult)
                K.tt(qraws[r], ps[:, b3, :], sinT, OP.mult)
                K.tt(dst, t1s[r], qraws[r], OP.add)
            hnr_step((p2a, p2b))

        w = wget(("wkv", b))
        wkv_t = w
        cidx = 0
        for oc in range(2):
            def mmk(bk, oc=oc):
                for kc in range(8):
                    K.mm(ps[:, bk, :], wkv_t[:, kc, oc * 128:(oc + 1) * 128], hkvT[:, kc, :], start=(kc == 0), stop=(kc == 7))
            hnr_p1(mmk, C("bk", oc), C("kn"), kT[:, oc, t0:t0 + TB], cidx % 3)
            cidx += 1
        for tl in range(4):
            bk = bank()
            for kc in range(8):
                K.mm(ps[:, bk, 0:256], hkvT[:, kc, tl * 128:(tl + 1) * 128], w[:, kc, 256:512], start=(kc == 0), stop=(kc == 7))
            K.tt(vtok[:, 4 * b + tl, :], ps[:, bk, 0:256], C("bv"), OP.add)
        qT_v = qT[:, 0:8, :].rearrange("p (pi i) t -> p pi i t", pi=2)
        attT_v = attT[:, 8:16, :].rearrange("p (pi i) t -> p pi i t", pi=2)
        apend = []
        npend = []
        cnts = [0, 0, cidx]

        def qproj(pi, wq_t, i_list):
            for i in i_list:
                def mmq(bk, i=i, wq_t=wq_t):
                    for kc in range(8):
                        K.mm(ps[:, bk, :], wq_t[:, kc, i * 128:(i + 1) * 128], hkvT[:, 8 + kc, :], start=(kc == 0), stop=(kc == 7))
                hnr_p1(mmq, C("bq", pi * 4 + i), C("qn"), qT_v[:, pi, i, :], cnts[2] % 3)
                cnts[2] += 1

        def core(pi):
            for nl in range(4):
                n = 4 * b + nl
                qs = slice(nl * 128, (nl + 1) * 128)
                bo_, bd_ = 4 + cnts[1] % 2, 6 + cnts[1] % 2
                cnts[1] += 1
                units = []
                for half in range(2):
                    tiles = ([(n - 1, CMv("mprev"))] if n > 0 else []) + [(n, CMv("mcur"))]
                    for ti, (kt, msk) in enumerate(tiles):
                        units.append((half, ti, kt, msk, len(tiles)))
                for ui, (half, ti, kt, msk, ntl) in enumerate(units):
                    g = 2 * pi + half
                    P0 = half * 64
                    ucnt = cnts[0]
                    bs_ = ucnt % 3
                    K.mm(ps[:, bs_, :].rearrange("p (i q) -> p i q", i=4), kT[P0:P0 + 64, pi, kt * 128:(kt + 1) * 128],
                         qT_v[P0:P0 + 64, pi, :, qs], start=True, stop=False)
                    K.mm(ps[:, bs_, :], ident, msk, start=False, stop=True)
                    while len(apend) > 1:
                        apend.pop(0)()

                    def fin(bs_=bs_, ucnt=ucnt, P0=P0, g=g, kt=kt, ti=ti, ntl=ntl, bo_=bo_, bd_=bd_, last_unit=(ui == len(units) - 1), pi=pi, qs=qs, last_block=(nl == 3)):
                        pt = PTb[:, ucnt % 4, :]
                        K.act(pt, ps[:, bs_, :], AF.Exp, scale=0.125)
                        first, last = (ti == 0), (ti == ntl - 1)
                        K.mm(ps[P0:P0 + 64, bo_, :], vtok[:, kt, g * 64:(g + 1) * 64], pt, start=first, stop=last)
                        K.mm(ps[P0:P0 + 64, bd_, :], ones[:, 0:64], pt, start=first, stop=last)
                        for fn_ in npend:
                            fn_()
                        npend.clear()
                        if last_unit:
                            def norm():
                                for i in range(4):
                                    K.ts(dtot[:, i * 128:(i + 1) * 128], ps[:, bd_, i * 128:(i + 1) * 128], esink[:, pi * 4 + i:pi * 4 + i + 1], None, OP.add)
                                if last_block:
                                    K.act(lnb[:], dtot, AF.Ln)
                                    K.act(dtot, lnb[:], AF.Exp, scale=-1.0)
                                else:
                                    K.add("dve", (lambda e: e.reciprocal(dtot, dtot)), [dtot], [dtot])
                                for i in range(4):
                                    K.tt(attT_v[:, pi, i, qs], ps[:, bo_, i * 128:(i + 1) * 128], dtot[:, i * 128:(i + 1) * 128], OP.mult)
                            npend.append(norm)
                    apend.append(fin)
                    cnts[0] += 1
            for f in apend:
                f()
            apend.clear()
            for fn_ in npend:
                fn_()
            npend.clear()

        wq0 = wget(("wq", b, 0))
        qproj(0, wq0, [0, 1, 2, 3])
        wq1 = wget(("wq", b, 1))
        qproj(1, wq1, [0, 1])
        core(0)
        qproj(1, wq1, [2, 3])
        hnr_step(None)
        hnr_step(None)
        core(1)
        if b == 0:
            dump("kT0", kT[:, :, 0:TB])
            dump("qT0", qT[:, 0:8, :])
            dump("v0", vtok[:, 0:4, :])
        if b == 0:
            dump("att0", attT[:, 8:16, :])
        for t in range(2):
            w = wget(("wo", b, t))
            banks = [bank() for _ in range(4)]
            for kc in range(8):
                for jj in range(4):
                    K.mm(ps[:, banks[jj], :], w[:, kc, jj * 128:(jj + 1) * 128], attT[:, 8 + kc, :], start=(kc == 0), stop=(kc == 7))
            for jj in range(4):
                oc = 4 * t + jj
                K.stt(xT[:, oc, :], ps[:, banks[jj], :], C("bo", oc), xT[:, oc, :], OP.add, OP.add)
        if b == 0:
            dump("x3T0", xT[:])
        if stage <= 4:
            continue
        ffn(1, b)
        xo_v = zsT[:].bitcast(F32).rearrange("p (a b) n -> p a (b n)", b=4)
        for tl in range(4):
            for c2 in range(2):
                bk = bank()
                for q in range(4):
                    c = 4 * c2 + q
                    K.tr(ps[:, bk, q * 128:(q + 1) * 128], xT[:, c, tl * 128:(tl + 1) * 128], identf)
                K.act(xo_v[:, tl, c2 * 512:(c2 + 1) * 512], ps[:, bk, :], AF.Copy)
            K.dma("sp", out_d[t0 + tl * 128:t0 + (tl + 1) * 128, :], xo_v[:, tl, :], f"out{tl}")

    finals = [f"out{tl}" for tl in range(4)] + ["dbg_" + n for n in dump_d]
    K.emit(finals)
    return K, dump_d


_CACHE = {}


def _prep_inputs(inputs):
    inp = {k: np.asarray(v) for k, v in inputs.items()}
    cst, cmb, cf = _host_consts(inp)
    shared = {
        "cst": cst, "cm": cmb, "cf": cf,
        "a_in_proj": np.ascontiguousarray(inp["a_in_proj"][0]),
        "a_out_proj": np.ascontiguousarray(inp["a_out_proj"][0]),
        "w_kv": np.ascontiguousarray(inp["w_kv"]),
        "w_q": np.ascontiguousarray(inp["w_q"][0]),
        "w_o": np.ascontiguousarray(inp["w_o"][0]),
        "f_w_in0": np.ascontiguousarray(inp["f_w_in"][0]),
        "f_w_in1": np.ascontiguousarray(inp["f_w_in"][1]),
        "f_w_down0": np.ascontiguousarray(inp["f_w_down"][0]),
        "f_w_down1": np.ascontiguousarray(inp["f_w_down"][1]),
    }
    maps = []
    for c in range(8):
        m = dict(shared)
        m["x"] = np.ascontiguousarray(inp["x"][c])
        m["pos"] = np.ascontiguousarray(inp["positions"][c].reshape(1, T).astype(np.int32))
        maps.append(m)
    return maps


def kernel(**inputs):
    maps = _prep_inputs(inputs)
    K, _ = build()
    res = run_bass_kernel_spmd(K.nc, maps, core_ids=list(range(8)))
    return np.stack([np.asarray(r["out"], dtype=np.float32) for r in res.results], axis=0)
```
